# Optimizing a Trainium2 kernel written in Bass

```python
import jax, jax.numpy as jnp
from jax import lax
import numpy as np

D_MODEL = 2048
BATCH = 2
SEQ = 8192
DEPTH = 1

GLA_HEADS = 4
GLA_DK = 128
GLA_DV = 256
GLA_GATE_RANK = 16
GLA_GATE_NORMALIZER = 16.0
HGRN_HEADS = 8
HGRN_DK = 128
HGRN_DV = 128
GLA_QK = GLA_HEADS * GLA_DK
GLA_V = GLA_HEADS * GLA_DV
HGRN_K = HGRN_HEADS * HGRN_DK
HGRN_V = HGRN_HEADS * HGRN_DV
MIX_WIDTH = GLA_V + HGRN_V
IN_SPLITS = (GLA_QK, GLA_QK, GLA_V, GLA_V, GLA_GATE_RANK, HGRN_K, HGRN_K, HGRN_V, HGRN_V)
IN_WIDTH = 2 * GLA_QK + 2 * GLA_V + GLA_GATE_RANK + 2 * HGRN_K + 2 * HGRN_V
CHUNK = 64
D_FF = 5504
MACARON_W = 0.5
N_MOD = 9
EPS = 1e-6

kernel_name = "hymba_gla_hgrn2_macaron_adaln"


def rmsnorm(x, w):
    xf = x.astype(jnp.float32)
    y = xf * lax.rsqrt(jnp.mean(xf * xf, axis=-1, keepdims=True) + EPS)
    return y.astype(x.dtype) * w


def modulate(h, shift, scale):
    return h * (1.0 + scale[:, None, :]) + shift[:, None, :]


def swiglu(h, wi, wo):
    a, b = jnp.split(h @ wi, 2, axis=-1)
    return (jax.nn.silu(a) * b) @ wo


def chunked_gated_linear_attention(q, k, v, log_a):
    B, H, T, dk = q.shape
    dv = v.shape[-1]
    n = T // CHUNK

    def to_chunks(t):
        t = t.astype(jnp.float32).reshape(B, H, n, CHUNK, t.shape[-1])
        return jnp.moveaxis(t, 2, 0)

    qc, kc, vc, ac = to_chunks(q), to_chunks(k), to_chunks(v), to_chunks(log_a)
    bc = jnp.cumsum(ac, axis=-2)
    causal = jnp.tril(jnp.ones((CHUNK, CHUNK), dtype=bool))[:, :, None]

    def step(S, inp):
        q_, k_, v_, b_ = inp
        o_inter = jnp.einsum('bhtk,bhkv->bhtv', q_ * jnp.exp(b_), S)
        rel = jnp.where(causal, b_[..., :, None, :] - b_[..., None, :, :], -jnp.inf)
        scores = jnp.einsum('bhtk,bhsk,bhtsk->bhts', q_, k_, jnp.exp(rel))
        o_intra = jnp.einsum('bhts,bhsv->bhtv', scores, v_)
        b_last = b_[..., -1:, :]
        S_new = jnp.exp(b_last[..., 0, :])[..., None] * S + jnp.einsum(
            'bhsk,bhsv->bhkv', k_ * jnp.exp(b_last - b_), v_)
        return S_new, o_inter + o_intra

    S0 = jnp.zeros((B, H, dk, dv), jnp.float32)
    _, o = lax.scan(step, S0, (qc, kc, vc, bc))
    return jnp.moveaxis(o, 0, 2).reshape(B, H, T, dv)


def head_rmsnorm(o, w):
    y = o * lax.rsqrt(jnp.mean(o * o, axis=-1, keepdims=True) + EPS)
    return y * w.astype(jnp.float32)


def token_mixer(h, w_in, gla_gate_w2, gla_gate_b2, gla_norm_w, hgrn_norm_w, lb, w_out):
    B, T, _ = h.shape
    z = h @ w_in
    split_at = [int(s) for s in np.cumsum(IN_SPLITS)[:-1]]
    gq, gk, gv, gg, gr, hq, hf, hi, hg = jnp.split(z, split_at, axis=-1)

    def heads(t, n_heads):
        return t.reshape(B, T, n_heads, -1).transpose(0, 2, 1, 3)

    def merge(o):
        return o.transpose(0, 2, 1, 3).reshape(B, T, -1).astype(h.dtype)

    g_log = jax.nn.log_sigmoid((gr @ gla_gate_w2 + gla_gate_b2).astype(jnp.float32)) / GLA_GATE_NORMALIZER
    o_gla = chunked_gated_linear_attention(
        heads(gq * (GLA_DK ** -0.5), GLA_HEADS), heads(gk, GLA_HEADS),
        heads(gv, GLA_HEADS), heads(g_log, GLA_HEADS))
    o_gla = merge(head_rmsnorm(o_gla, gla_norm_w)) * jax.nn.silu(gg)

    f_raw = hf.astype(jnp.float32)
    log_f = jnp.logaddexp(jnp.log(lb), jnp.log1p(-lb) + jax.nn.log_sigmoid(f_raw))
    k_in = -jnp.expm1(log_f)
    o_hg = chunked_gated_linear_attention(
        heads(jax.nn.silu(hq), HGRN_HEADS), heads(k_in, HGRN_HEADS),
        heads(hi, HGRN_HEADS), heads(log_f, HGRN_HEADS))
    o_hg = merge(head_rmsnorm(o_hg, hgrn_norm_w)) * jax.nn.silu(hg)

    return jnp.concatenate([o_gla, o_hg], axis=-1) @ w_out


def setup_inputs(seed: int = 0) -> dict:
    key = jax.random.key(seed)
    ks = jax.random.split(key, 24)
    f32 = jnp.float32

    def nrm(k, shape, scale):
        return jax.random.normal(k, shape, f32) * scale

    def gain(k, shape):
        return jnp.ones(shape, f32) + 0.02 * jax.random.normal(k, shape, f32)

    L, D = DEPTH, D_MODEL
    return {
        "x": nrm(ks[0], (BATCH, SEQ, D), 1.0),
        "c": nrm(ks[1], (BATCH, D), 1.0),
        "ada_w": nrm(ks[2], (L, D, N_MOD * D), 0.5 * D ** -0.5),
        "ada_b": nrm(ks[3], (L, N_MOD * D), 0.02),
        "norm_ffn1_w": gain(ks[4], (L, D)),
        "ffn1_wi": nrm(ks[5], (L, D, 2 * D_FF), D ** -0.5),
        "ffn1_wo": nrm(ks[6], (L, D_FF, D), D_FF ** -0.5),
        "norm_mix_w": gain(ks[7], (L, D)),
        "w_in": nrm(ks[8], (L, D, IN_WIDTH), D ** -0.5),
        "gla_gate_w2": nrm(ks[9], (L, GLA_GATE_RANK, GLA_QK), GLA_GATE_RANK ** -0.5),
        "gla_gate_b2": nrm(ks[10], (L, GLA_QK), 0.02),
        "gla_norm_w": gain(ks[11], (L, GLA_DV)),
        "hgrn_norm_w": gain(ks[12], (L, HGRN_DV)),
        "hgrn_lower_bounds": gain(ks[13], (L + 1, HGRN_K)),
        "w_out": nrm(ks[14], (L, MIX_WIDTH, D), MIX_WIDTH ** -0.5),
        "norm_ffn2_w": gain(ks[15], (L, D)),
        "ffn2_wi": nrm(ks[16], (L, D, 2 * D_FF), D ** -0.5),
        "ffn2_wo": nrm(ks[17], (L, D_FF, D), D_FF ** -0.5),
        "final_norm_w": gain(ks[18], (D,)),
    }


def reference(x, c, ada_w, ada_b, norm_ffn1_w, ffn1_wi, ffn1_wo, norm_mix_w, w_in,
              gla_gate_w2, gla_gate_b2, gla_norm_w, hgrn_norm_w, hgrn_lower_bounds,
              w_out, norm_ffn2_w, ffn2_wi, ffn2_wo, final_norm_w):
    lb_table = jnp.cumsum(jax.nn.softmax(hgrn_lower_bounds.astype(jnp.float32), axis=0), axis=0)
    c_act = jax.nn.silu(c)
    for l in range(DEPTH):
        mod = c_act @ ada_w[l] + ada_b[l]
        sh1, sc1, g1, sh2, sc2, g2, sh3, sc3, g3 = jnp.split(mod, N_MOD, axis=-1)
        h = modulate(rmsnorm(x, norm_ffn1_w[l]), sh1, sc1)
        x = x + MACARON_W * g1[:, None, :] * swiglu(h, ffn1_wi[l], ffn1_wo[l])
        h = modulate(rmsnorm(x, norm_mix_w[l]), sh2, sc2)
        x = x + g2[:, None, :] * token_mixer(h, w_in[l], gla_gate_w2[l], gla_gate_b2[l],
                                              gla_norm_w[l], hgrn_norm_w[l], lb_table[l], w_out[l])
        h = modulate(rmsnorm(x, norm_ffn2_w[l]), sh3, sc3)
        x = x + MACARON_W * g3[:, None, :] * swiglu(h, ffn2_wi[l], ffn2_wo[l])
    return rmsnorm(x, final_norm_w)
```

```python
import numpy as np
import concourse.bass as bass
import concourse.mybir as mybir
from concourse.bass_utils import run_bass_kernel_spmd
from contextlib import ExitStack

F32 = mybir.dt.float32
BF16 = mybir.dt.bfloat16
AF = mybir.ActivationFunctionType
ALU = mybir.AluOpType
AX = mybir.AxisListType

D = 2048
KC = 16
TCORE = 2048
TT = 512
NT = TCORE // TT
FF = 5504
FC = 43
INW = 7184
CH = 128
NCH = TT // CH
EPS = 1e-6
NSLOT = 3
SW = 2048

P_C = 0
P_ADAB = P_C + 16
P_NW1 = P_ADAB + 144
P_NW2 = P_NW1 + 16
P_NW3 = P_NW2 + 16
P_NWF = P_NW3 + 16
P_B2 = P_NWF + 16
P_GNW = P_B2 + 4
P_HNW = P_GNW + 2
P_LB0 = P_HNW + 1
P_LB1 = P_LB0 + 8
P_PM = P_LB1 + 8
NPRE = 12
NPAR = P_PM + NPRE
C_ID = 0
C_MU = 128
C_ONE = 256
C_SC = 384
NCST = 384 + 512


class Sched:
    def __init__(self):
        self.ops = []
        self.last_w = {}
        self.readers = {}

    def op(self, eng, fn, reads=(), writes=(), dma=None, nodep=()):
        oid = len(self.ops)
        deps = {}
        for k in reads:
            w = self.last_w.get(k)
            if w is not None:
                deps[w] = True
        for k in writes:
            w = self.last_w.get(k)
            if w is not None:
                deps[w] = True
            for r in self.readers.get(k, ()):
                if r not in deps:
                    deps[r] = False
        for k in reads:
            self.readers.setdefault(k, []).append(oid)
        for k in writes:
            self.last_w[k] = oid
            self.readers[k] = []
        deps.pop(oid, None)
        for n in nodep:
            deps.pop(n, None)
        self.ops.append(dict(eng=eng, fn=fn, deps=deps, dma=dma))
        return oid

    def emit(self, nc, stack):
        ops = self.ops
        signaled = set()
        for o in ops:
            signaled.update(o["deps"].keys())
        sems = {}

        def sem(key):
            if key not in sems:
                sems[key] = stack.enter_context(nc.semaphore("s%d" % len(sems)))
            return sems[key]

        cnt = {}
        for i, o in enumerate(ops):
            if o["dma"] is not None:
                k = ("dma", o["dma"])
                cnt[k] = cnt.get(k, 0) + 16
                o["tok"] = (k, cnt[k], 16)
            elif i in signaled:
                k = ("eng", o["eng"])
                cnt[k] = cnt.get(k, 0) + 1
                o["tok"] = (k, cnt[k], 1)
            else:
                o["tok"] = None
        for k in cnt:
            sem(k)
        per = {}
        for i, o in enumerate(ops):
            per.setdefault(o["eng"], []).append(i)
        block = stack.enter_context(nc.Block())

        def run(engname, e):
            seen = {}
            for i in per.get(engname, []):
                o = ops[i]
                for d in sorted(o["deps"].keys()):
                    od = ops[d]
                    flow = o["deps"][d]
                    if od["dma"] is None and od["eng"] == engname:
                        if engname == "pe" or not flow:
                            continue
                    k, v, _ = od["tok"]
                    if seen.get(k, 0) < v:
                        e.wait_ge(sems[k], v)
                        seen[k] = v
                if o["fn"] is None:
                    continue
                inst = o["fn"](e)
                if o["tok"] is not None:
                    k, v, amt = o["tok"]
                    inst.then_inc(sems[k], amt)

        @block.sync
        def _(e):
            run("sp", e)

        @block.scalar
        def _(e):
            run("act", e)

        @block.vector
        def _(e):
            run("dve", e)

        @block.gpsimd
        def _(e):
            run("pool", e)

        @block.tensor
        def _(e):
            run("pe", e)


def build(stage=99):
    nc = bass.Bass("TRN2", target_bir_lowering=False)

    def dram(name, shape, dt=F32, kind="ExternalInput"):
        return nc.dram_tensor(name, shape, dt, kind=kind).ap()

    x_d = dram("x", [TCORE, D])
    xp_d = dram("xp", [NPRE * TT, D])
    par_d = dram("par", [128, NPAR])
    cst_d = dram("cst", [128, NCST])
    adaw_d = dram("ada_w", [D, 9 * D])
    wi_d = [dram("wi1", [D, 2 * FF]), dram("wi2", [D, 2 * FF])]
    wo_d = [dram("wo1", [FF, D]), dram("wo2", [FF, D])]
    win_d = dram("w_in", [D, INW])
    w2_d = dram("w2", [16, 512])
    wout_d = dram("w_out", [D, D])
    y_d = dram("y", [TCORE, D], kind="ExternalOutput")

    NWC = 128
    wsc_d = dram("wsc", [NWC * 128, 8192], BF16, kind="Internal")

    adaw_v = adaw_d.rearrange("(kc p) n -> p kc n", p=128)
    wi_v = [w.rearrange("(kc p) (two f) -> p kc two f", p=128, two=2) for w in wi_d]
    wo_v = [w.rearrange("(fc p) d -> p fc d", p=128) for w in wo_d]
    win_v = win_d.rearrange("(kc p) n -> p kc n", p=128)
    wout_v = wout_d.rearrange("(kc p) n -> p kc n", p=128)

    S = Sched()
    stack = ExitStack()
    with stack:
        def sb(name, shape, dt=F32):
            return stack.enter_context(nc.sbuf_tensor("sb_" + name, shape, dt))

        def pst(name, shape, dt=F32):
            return stack.enter_context(nc.psum_tensor("pp_" + name, shape, dt))

        par = sb("par", [128, NPAR])
        cst = sb("cst", [128, NCST])
        identb = sb("identb", [128, 128], BF16)
        maskb = sb("maskb", [128, 128], BF16)
        onesb = sb("onesb", [128, 128], BF16)
        cact = sb("cact", [128, 16])
        mod = sb("mod", [128, 144])
        der = sb("der", [128, 5, 16])
        lbv = sb("lbv", [128, 3, 8])
        nb2 = sb("nb2", [128, 4])
        xT = sb("xT", [128, KC, TT])
        hT = sb("hT", [128, KC, TT], BF16)
        arena = sb("arena", [128, FC * TT], BF16)
        wsl = [sb("wsl%d" % i, [128, 8192], BF16) for i in range(NSLOT)]
        stg = [sb("stg%d" % i, [128, D]) for i in range(2)]
        rstd = sb("rstd", [128, TT])
        tA = sb("tA", [128, TT])
        tB = sb("tB", [128, TT])
        sqb = [sb("sqb%d" % i, [128, TT], BF16) for i in range(2)]
        grT = sb("grT", [16, TT], BF16)
        w2b = sb("w2b", [16, 512], BF16)
        wgr = sb("wgr", [128, KC, 16], BF16)
        cl = sb("cl", [128, TT])
        eq = sb("eq", [128, TT])
        ek = sb("ek", [128, TT])
        u1 = sb("u1", [128, TT])
        u2 = sb("u2", [128, TT])
        qt = sb("qt", [128, TT], BF16)
        kt = sb("kt", [128, TT], BF16)
        ktok = sb("ktok", [128, NCH, 128], BF16)
        PT = [sb("PT%d" % i, [128, 128], BF16) for i in range(2)]
        Sall = sb("Sall", [128, SW])
        Sbf = sb("Sbf", [128, 2048], BF16)
        Tst = sb("Tst", [128, 256])
        tiny = sb("tiny", [128, 16])

        ps = [pst("ps%d" % i, [128, 512]) for i in range(7)]
        pstr = pst("pstr", [128, 1024], BF16)

        gT = arena[:, :].rearrange("p (f t) -> p f t", t=TT)
        vtok = arena[:, 0:8192].rearrange("p (s v) -> p s v", v=2048)
        ogT = arena[:, 8192:16384].rearrange("p (c t) -> p c t", t=TT)

        def akeys(lo, hi):
            return [("ar", i) for i in range(lo // 512, (hi + 511) // 512)]

        ident = cst[:, C_ID:C_ID + 128]
        scanm = cst[:, C_SC:C_SC + 512]

        def act(out, in_, func, reads, writes, bias=None, scale=None):
            kw = {}
            if bias is not None:
                kw["bias"] = bias
            if scale is not None:
                kw["scale"] = scale
            S.op("act", lambda e: e.activation(out=out, in_=in_, func=func, **kw), reads, writes)

        def tt(out, in0, in1, op, reads, writes, eng="dve"):
            S.op(eng, lambda e: e.tensor_tensor(out=out, in0=in0, in1=in1, op=op), reads, writes)

        def ts(out, in0, s1, s2, op0, op1, reads, writes, eng="dve"):
            if op1 is None:
                S.op(eng, lambda e: e.tensor_scalar(out=out, in0=in0, scalar1=s1, scalar2=None, op0=op0), reads, writes)
            else:
                S.op(eng, lambda e: e.tensor_scalar(out=out, in0=in0, scalar1=s1, scalar2=s2, op0=op0, op1=op1), reads, writes)

        def stt(out, in0, scalar, in1, op0, op1, reads, writes):
            S.op("dve", lambda e: e.scalar_tensor_tensor(out=out, in0=in0, scalar=scalar, in1=in1, op0=op0, op1=op1), reads, writes)

        def cp(out, in_, reads, writes, eng="dve"):
            if eng == "act":
                S.op("act", lambda e: e.copy(out=out, in_=in_), reads, writes)
            else:
                S.op(eng, lambda e: e.tensor_copy(out=out, in_=in_), reads, writes)

        def mmgroup(items, reads, writes):
            def fn(e):
                inst = None
                for (o, l, r, st, sp) in items:
                    inst = e.matmul(o, l, r, start=st, stop=sp)
                return inst
            S.op("pe", fn, reads, writes)

        def transposes(items, reads, writes):
            def fn(e):
                inst = None
                for (o, i, idn) in items:
                    inst = e.transpose(o, i, idn)
                return inst
            S.op("pe", fn, reads, writes)

        def dma(q, out, in_, reads, writes, tag, nodep=()):
            return S.op(q, lambda e: e.dma_start(out=out, in_=in_), reads, writes, dma=tag, nodep=nodep)

        wstate = dict(n=0)
        wcache = {}

        def wload(segs, ckey=None, used=8192):
            slot = wstate["n"] % NSLOT
            wstate["n"] += 1
            key = ("w", slot)
            if ckey is not None and ckey in wcache:
                idx = wcache[ckey]
                dma("sp", wsl[slot][:, 0:used], wsc_d[idx * 128:(idx + 1) * 128, 0:used], [("wc", idx)], [key], ("w", slot))
                return slot, key
            ids = []
            for (dstf, src) in segs:
                ids.append(dma("pool", dstf(wsl[slot]), src, [], [key], ("w", slot), nodep=ids))
            if ckey is not None:
                idx = len(wcache)
                assert idx < NWC
                wcache[ckey] = idx
                dma("sp", wsc_d[idx * 128:(idx + 1) * 128, 0:used], wsl[slot][:, 0:used], [key], [("wc", idx)], ("wcs", slot))
            return slot, key

        dma("sp", par[:, :], par_d[:, :], [], ["par"], "par")
        dma("sp", cst[:, :], cst_d[:, :], [], ["cst"], "cst")
        dma("pool", w2b[:, :], w2_d[:, :], [], ["w2b"], "w2b")
        dma("pool", wgr[:, :, :], win_v[:, :, 3072:3088], [], ["wgr"], "wgr")
        cp(identb[:, :], cst[:, C_ID:C_ID + 128], ["cst"], ["identb"])
        cp(maskb[:, :], cst[:, C_MU:C_MU + 128], ["cst"], ["maskb"])
        cp(onesb[:, :], cst[:, C_ONE:C_ONE + 128], ["cst"], ["onesb"])
        act(cact[:, :], par[:, P_C:P_C + 16], AF.Silu, ["par"], ["cact"])
        for j in range(144):
            slot = j % 2
            sv = stg[slot][:, :].rearrange("p (kc n) -> p kc n", n=128)
            dma("sp", sv, adaw_v[:, :, j * 128:(j + 1) * 128], [], [("stg", slot)], ("stg", slot))
            mmgroup([(ps[0][:, j:j + 1], sv[:, kc, :], cact[:, kc:kc + 1], kc == 0, kc == KC - 1) for kc in range(KC)],
                    [("stg", slot), "cact"], [("ps", 0)])
        tt(mod[:, :], ps[0][:, 0:144], par[:, P_ADAB:P_ADAB + 144], ALU.add, [("ps", 0), "par"], ["mod"])

        def modc(m):
            return mod[:, m * 16:(m + 1) * 16]
        for i, (pn, m) in enumerate([(P_NW1, 1), (P_NW2, 4), (P_NW3, 7)]):
            stt(der[:, i, :], modc(m), 1.0, par[:, pn:pn + 16], ALU.add, ALU.mult, ["mod", "par"], ["der"])
        ts(der[:, 3, :], modc(2), 0.5, None, ALU.mult, None, ["mod"], ["der"])
        ts(der[:, 4, :], modc(8), 0.5, None, ALU.mult, None, ["mod"], ["der"])
        tt(tiny[:, 0:8], par[:, P_LB0:P_LB0 + 8], par[:, P_LB1:P_LB1 + 8], ALU.subtract, ["par"], ["tiny"])
        act(lbv[:, 0, :], tiny[:, 0:8], AF.Sigmoid, ["tiny"], ["lbv"])
        ts(lbv[:, 1, :], lbv[:, 0, :], -1.0, 1.0, ALU.mult, ALU.add, ["lbv"], ["lbv"])
        ts(lbv[:, 2, :], lbv[:, 0, :], -1.0, None, ALU.add, None, ["lbv"], ["lbv"])
        ts(nb2[:, :], par[:, P_B2:P_B2 + 4], -1.0, None, ALU.mult, None, ["par"], ["nb2"])
        S.op("dve", lambda e: e.memset(Sall[:, :], 0.0), [], ["Sall"])
        S.op("dve", lambda e: e.memset(Sbf[:, :], 0.0), [], ["Sbf"])

        def norm_mod(wm, sh):
            for c in range(KC):
                act(sqb[c % 2][:, :], xT[:, c, :], AF.Square, ["xT"], [("sqb", c % 2)])
                mmgroup([(ps[6][:, :], onesb[:, :], sqb[c % 2][:, :], c == 0, c == KC - 1)],
                        [("sqb", c % 2), "onesb"], [("ps", 6)])
            compute_rstd(ps[6], 1.0 / D, ("ps", 6))
            for c in range(KC):
                t = tA if c % 2 == 0 else tB
                tk = "tA" if c % 2 == 0 else "tB"
                tt(t[:, :], xT[:, c, :], rstd[:, :], ALU.mult, ["xT", "rstd"], [tk])
                act(hT[:, c, :], t[:, :], AF.Identity, [tk, "der", "mod"], ["hT"], bias=sh[:, c:c + 1], scale=wm[:, c:c + 1])

        def compute_rstd(psb, inv_n, pskey):
            ts(rstd[:, :], psb[:, :], inv_n, EPS, ALU.mult, ALU.add, [pskey], ["rstd"])
            act(rstd[:, :], rstd[:, :], AF.Sqrt, ["rstd"], ["rstd"])
            S.op("dve", lambda e: e.reciprocal(out=rstd[:, :], in_=rstd[:, :]), ["rstd"], ["rstd"])

        def ffn(fi, gate_half):
            ftiles = [(f0, 256) for f0 in range(0, 5376, 256)] + [(5376, 128)]
            pend = []

            def issue(k):
                f0, w = ftiles[k]
                pend.append(wload([(lambda s, w=w: s[:, 0:KC * w].rearrange("p (kc f) -> p kc f", f=w),
                                    wi_v[fi][:, :, 0, f0:f0 + w]),
                                   (lambda s, w=w: s[:, KC * w:2 * KC * w].rearrange("p (kc f) -> p kc f", f=w),
                                    wi_v[fi][:, :, 1, f0:f0 + w])], ckey=("wi", fi, k), used=2 * KC * w))
            issue(0)
            issue(1)
            cidx = 0
            for k, (f0, w) in enumerate(ftiles):
                if k + 2 < len(ftiles):
                    issue(k + 2)
                slot, key = pend[k]
                wva = wsl[slot][:, 0:KC * w].rearrange("p (kc f) -> p kc f", f=w)
                wvb = wsl[slot][:, KC * w:2 * KC * w].rearrange("p (kc f) -> p kc f", f=w)
                for j in range(w // 128):
                    f = f0 // 128 + j
                    pa = ps[(cidx % 2) * 2]
                    pb = ps[(cidx % 2) * 2 + 1]
                    ka = ("ps", (cidx % 2) * 2)
                    kb = ("ps", (cidx % 2) * 2 + 1)
                    cidx += 1
                    mmgroup([(pa[:, :], wva[:, kc, j * 128:(j + 1) * 128], hT[:, kc, :], kc == 0, kc == KC - 1) for kc in range(KC)],
                            [key, "hT"], [ka])
                    mmgroup([(pb[:, :], wvb[:, kc, j * 128:(j + 1) * 128], hT[:, kc, :], kc == 0, kc == KC - 1) for kc in range(KC)],
                            [key, "hT"], [kb])
                    t = tA if f % 2 == 0 else tB
                    tk = "tA" if f % 2 == 0 else "tB"
                    act(t[:, :], pa[:, :], AF.Silu, [ka], [tk])
                    tt(gT[:, f, :], t[:, :], pb[:, :], ALU.mult, [tk, kb], [("ar", f)])
            pend2 = []

            def issue2(d):
                pend2.append(wload([(lambda s: s[:, 0:FC * 128].rearrange("p (fc n) -> p fc n", n=128),
                                     wo_v[fi][:, :, d * 128:(d + 1) * 128])], ckey=("wo", fi, d), used=FC * 128))
            issue2(0)
            issue2(1)
            for d in range(KC):
                if d + 2 < KC:
                    issue2(d + 2)
                slot, key = pend2[d]
                wv = wsl[slot][:, 0:FC * 128].rearrange("p (fc n) -> p fc n", n=128)
                po = ps[4 + d % 2]
                kp = ("ps", 4 + d % 2)
                mmgroup([(po[:, :], wv[:, f, :], gT[:, f, :], f == 0, f == FC - 1) for f in range(FC)],
                        [key] + [("ar", f) for f in range(FC)], [kp])
                stt(xT[:, d, :], po[:, :], gate_half[:, d:d + 1], xT[:, d, :], ALU.mult, ALU.add, [kp, "xT", "der", "mod"], ["xT"])

        def load_x(t, src=None):
            src = x_d if src is None else src
            for s in range(4):
                slot = s % 2
                r0 = t * TT + s * 128
                dma("sp", stg[slot][:, :], src[r0:r0 + 128, :], [], [("stg", slot)], ("stg", slot))
                for g in range(4):
                    transposes([(ps[g][:, i * 128:(i + 1) * 128], stg[slot][:, (g * 4 + i) * 128:(g * 4 + i + 1) * 128], ident) for i in range(4)],
                               [("stg", slot), "cst"], [("ps", g)])
                    cp(xT[:, g * 4:(g + 1) * 4, s * 128:(s + 1) * 128],
                       ps[g][:, :].rearrange("p (c t) -> p c t", t=128), [("ps", g)], ["xT"],
                       eng=("act" if g % 2 == 0 else "dve"))

        def final_out(t):
            for c in range(KC):
                act(sqb[c % 2][:, :], xT[:, c, :], AF.Square, ["xT"], [("sqb", c % 2)])
                mmgroup([(ps[6][:, :], onesb[:, :], sqb[c % 2][:, :], c == 0, c == KC - 1)],
                        [("sqb", c % 2), "onesb"], [("ps", 6)])
            compute_rstd(ps[6], 1.0 / D, ("ps", 6))
            for c in range(KC):
                stt(xT[:, c, :], xT[:, c, :], par[:, P_NWF + c:P_NWF + c + 1], rstd[:, :], ALU.mult, ALU.mult, ["xT", "par", "rstd"], ["xT"])
            for s in range(4):
                slot = s % 2
                for g in range(4):
                    transposes([(ps[g][:, i * 128:(i + 1) * 128], xT[:, g * 4 + i, s * 128:(s + 1) * 128], ident) for i in range(4)],
                               ["xT", "cst"], [("ps", g)])
                    cp(stg[slot][:, g * 512:(g + 1) * 512], ps[g][:, :], [("ps", g)], [("stg", slot)],
                       eng=("act" if g % 2 == 0 else "dve"))
                r0 = t * TT + s * 128
                dma("sp", y_d[r0:r0 + 128, :], stg[slot][:, :], [("stg", slot)], [("y", t, s)], ("stg", slot))

        def head_cols(h):
            if h < 4:
                return [(h * 128, 128), (512 + h * 128, 128), (2048 + h * 256, 256)]
            hh = h - 4
            return [(3088 + hh * 128, 128), (4112 + hh * 128, 128), (6160 + hh * 128, 128)]

        def head_geom(h):
            if h < 4:
                return 256, h * 256
            return 128, 1024 + (h - 4) * 128

        def mixer(t, pass1, mcol=None):
            mmgroup([(ps[2][0:16, :], wgr[:, kc, :], hT[:, kc, :], kc == 0, kc == KC - 1) for kc in range(KC)],
                    ["wgr", "hT"], [("ps", 2)])
            cp(grT[:, :], ps[2][0:16, :], [("ps", 2)], ["grT"])
            vgroups = [1024, 1536, 5136, 5648]
            pend = []

            def issue(i):
                c0 = vgroups[i]
                pend.append(wload([(lambda s: s[:, 0:KC * 512].rearrange("p (kc n) -> p kc n", n=512), win_v[:, :, c0:c0 + 512])], ckey=("v", i)))
            issue(0)
            issue(1)
            n = 0
            for i in range(4):
                if i + 2 < 4:
                    issue(i + 2)
                slot, key = pend[i]
                wv = wsl[slot][:, 0:KC * 512].rearrange("p (kc n) -> p kc n", n=512)
                for s in range(4):
                    pb = ps[n % 2]
                    kb = ("ps", n % 2)
                    mmgroup([(pb[:, :], hT[:, kc, s * 128:(s + 1) * 128], wv[:, kc, :], kc == 0, kc == KC - 1) for kc in range(KC)],
                            [key, "hT"], [kb])
                    lo = s * 2048 + i * 512
                    cp(vtok[:, s, i * 512:(i + 1) * 512], pb[:, :], [kb], akeys(lo, lo + 512), eng=("act" if n % 2 == 0 else "dve"))
                    n += 1
            hp = []

            def hissue(h):
                segs = []
                off = 0
                for si, (c0, w) in enumerate(head_cols(h)):
                    segs.append((lambda s, off=off, w=w: s[:, 0:KC * 512].rearrange("p (kc n) -> p kc n", n=512)[:, :, off:off + w],
                                 win_v[:, :, c0:c0 + w]))
                    off += w
                hp.append(wload(segs, ckey=("h2", h)))
            if pass1:
                kt1 = [wload([(lambda s: s[:, 0:KC * 512].rearrange("p (kc n) -> p kc n", n=512), win_v[:, :, c0:c0 + 512])], ckey=("k1", c0))
                       for c0 in (512, 4112, 4624)]
            else:
                hissue(0)
                hissue(1)
            for h in range(12):
                if pass1:
                    ti = 0 if h < 4 else 1 + (h - 4) // 4
                    slot, key = kt1[ti]
                    koff = (h % 4) * 128
                else:
                    if h + 2 < 12:
                        hissue(h + 2)
                    slot, key = hp[h]
                    koff = 128
                wv = wsl[slot][:, 0:KC * 512].rearrange("p (kc n) -> p kc n", n=512)
                dv, cb = head_geom(h)
                nj = dv // 128
                gla = h < 4
                Sh = Sall[:, cb:cb + dv]
                Shb = Sbf[:, cb:cb + dv]
                sk = ("S", h)
                if not pass1:
                    mmgroup([(ps[0][:, :], wv[:, kc, 0:128], hT[:, kc, :], kc == 0, kc == KC - 1) for kc in range(KC)],
                            [key, "hT"], [("ps", 0)])
                mmgroup([(ps[1][:, :], wv[:, kc, koff:koff + 128], hT[:, kc, :], kc == 0, kc == KC - 1) for kc in range(KC)],
                        [key, "hT"], [("ps", 1)])
                if gla:
                    mmgroup([(ps[2][:, :], w2b[:, h * 128:(h + 1) * 128], grT[:, :], True, True)], ["w2b", "grT"], [("ps", 2)])
                    act(u1[:, :], ps[2][:, :], AF.Exp, [("ps", 2), "nb2"], ["u1"], bias=nb2[:, h:h + 1], scale=-1.0)
                    act(u2[:, :], u1[:, :], AF.Ln, ["u1"], ["u2"], bias=1.0)
                    S.op("dve", lambda e: e.tensor_tensor_scan(out=cl[:, :], data0=scanm, data1=u2[:, :], initial=0.0, op0=ALU.mult, op1=ALU.add),
                         ["cst", "u2"], ["cl"])
                    qs, ks = -1.0 / 16.0, 1.0 / 16.0
                else:
                    hh = h - 4
                    act(u1[:, :], ps[1][:, :], AF.Sigmoid, [("ps", 1)], ["u1"])
                    act(u2[:, :], u1[:, :], AF.Ln, ["u1", "lbv"], ["u2"], bias=lbv[:, 0, hh:hh + 1], scale=lbv[:, 1, hh:hh + 1])
                    S.op("dve", lambda e: e.tensor_tensor_scan(out=cl[:, :], data0=scanm, data1=u2[:, :], initial=0.0, op0=ALU.mult, op1=ALU.add),
                         ["cst", "u2"], ["cl"])
                    qs, ks = 1.0, -1.0
                act(ek[:, :], cl[:, :], AF.Exp, ["cl"], ["ek"], scale=ks)
                act(eq[:, :], cl[:, :], AF.Exp, ["cl"], ["eq"], scale=qs)
                if gla:
                    if pass1:
                        stt(kt[:, :], ps[1][:, :], mcol, ek[:, :], ALU.mult, ALU.mult, [("ps", 1), "ek", "par"], ["kt"])
                    else:
                        tt(kt[:, :], ps[1][:, :], ek[:, :], ALU.mult, [("ps", 1), "ek"], ["kt"])
                    if not pass1:
                        stt(qt[:, :], ps[0][:, :], 128.0 ** -0.5, eq[:, :], ALU.mult, ALU.mult, [("ps", 0), "eq"], ["qt"])
                else:
                    ts(u1[:, :], u1[:, :], lbv[:, 2, hh:hh + 1], lbv[:, 1, hh:hh + 1], ALU.mult, ALU.add, ["u1", "lbv"], ["u1"])
                    if pass1:
                        stt(kt[:, :], u1[:, :], mcol, ek[:, :], ALU.mult, ALU.mult, ["u1", "ek", "par"], ["kt"])
                    else:
                        tt(kt[:, :], u1[:, :], ek[:, :], ALU.mult, ["u1", "ek"], ["kt"])
                    if not pass1:
                        act(u2[:, :], ps[0][:, :], AF.Silu, [("ps", 0)], ["u2"])
                        tt(qt[:, :], u2[:, :], eq[:, :], ALU.mult, ["u2", "eq"], ["qt"])
                transposes([(pstr[:, c * 128:(c + 1) * 128], kt[:, c * 128:(c + 1) * 128], identb[:, :]) for c in range(NCH)],
                           ["kt", "identb"], ["pstr"])
                cp(ktok[:, :, :], pstr[:, 0:NCH * 128].rearrange("p (c n) -> p c n", n=128), ["pstr"], ["ktok"], eng="act")
                po = [ps[3], ps[4]]
                pok = [("ps", 3), ("ps", 4)]
                for c in range(NCH):
                    csl = slice(c * CH, (c + 1) * CH)
                    vch = vtok[:, c, cb:cb + dv]
                    vkeys = akeys(c * 2048 + cb, c * 2048 + cb + dv)
                    if not pass1:
                        mmgroup([(ps[5][:, 0:128], kt[:, csl], qt[:, csl], True, True)], ["kt", "qt"], [("ps", 5)])
                        P = PT[c % 2]
                        pk = ("PT", c % 2)
                        tt(P[:, :], ps[5][:, 0:128], maskb[:, :], ALU.mult, [("ps", 5), "maskb"], [pk])
                        for j in range(nj):
                            mmgroup([(po[j][:, csl], vch[:, j * 128:(j + 1) * 128], P[:, :], True, False),
                                     (po[j][:, csl], Shb[:, j * 128:(j + 1) * 128], qt[:, csl], False, True)],
                                    vkeys + [pk, ("Sb", h), "qt"], [pok[j]])
                    mmgroup([(ps[6][:, 0:dv], ktok[:, c, :], vch, True, True)], ["ktok"] + vkeys, [("ps", 6)])
                    a_ap = eq[:, c * CH + CH - 1:c * CH + CH]
                    tt(Tst[:, 0:dv], ps[6][:, 0:dv], Sh, ALU.add, [("ps", 6), sk], ["Tst"])
                    ts(Sh, Tst[:, 0:dv], a_ap, None, ALU.mult, None, ["Tst", "eq"], [sk])
                    act(Shb, Tst[:, 0:dv], AF.Identity, ["Tst", "eq"], [("Sb", h)], scale=a_ap)
                if pass1:
                    continue
                for j in range(nj):
                    act(sqb[j][:, :], po[j][:, :], AF.Square, [pok[j]], [("sqb", j)])
                    mmgroup([(ps[5][:, :], onesb[:, :], sqb[j][:, :], j == 0, j == nj - 1)], [("sqb", j), "onesb"], [("ps", 5)])
                compute_rstd(ps[5], 1.0 / dv, ("ps", 5))
                for j in range(nj):
                    pg = ps[j]
                    kg = ("ps", j)
                    mmgroup([(pg[:, :], wv[:, kc, 256 + j * 128:256 + (j + 1) * 128], hT[:, kc, :], kc == 0, kc == KC - 1) for kc in range(KC)],
                            [key, "hT"], [kg])
                    act(u1[:, :], pg[:, :], AF.Silu, [kg], ["u1"])
                    tt(u2[:, :], po[j][:, :], rstd[:, :], ALU.mult, [pok[j], "rstd"], ["u2"])
                    nwc = par[:, P_GNW + j:P_GNW + j + 1] if gla else par[:, P_HNW:P_HNW + 1]
                    fc = cb // 128 + j
                    stt(ogT[:, fc, :], u2[:, :], nwc, u1[:, :], ALU.mult, ALU.mult, ["u2", "u1", "par"], akeys(8192 + fc * 512, 8192 + fc * 512 + 512))
            if pass1:
                return
            pw = []

            def wissue(i):
                pw.append(wload([(lambda s: s[:, 0:KC * 512].rearrange("p (kc n) -> p kc n", n=512), wout_v[:, :, i * 512:(i + 1) * 512])], ckey=("wout", i)))
            wissue(0)
            wissue(1)
            for i in range(4):
                if i + 2 < 4:
                    wissue(i + 2)
                slot, key = pw[i]
                wv = wsl[slot][:, 0:KC * 512].rearrange("p (kc n) -> p kc n", n=512)
                for jj in range(4):
                    d = i * 4 + jj
                    pb = ps[d % 2]
                    kb = ("ps", d % 2)
                    mmgroup([(pb[:, :], wv[:, kc, jj * 128:(jj + 1) * 128], ogT[:, kc, :], kc == 0, kc == KC - 1) for kc in range(KC)],
                            [key] + akeys(8192, 16384), [kb])
                    stt(xT[:, d, :], pb[:, :], mod[:, 5 * 16 + d:5 * 16 + d + 1], xT[:, d, :], ALU.mult, ALU.add, [kb, "xT", "mod"], ["xT"])

        npre = NPRE if stage >= 2 else 0
        for p in range(npre):
            load_x(p, xp_d)
            norm_mod(der[:, 0, :], modc(0))
            ffn(0, der[:, 3, :])
            norm_mod(der[:, 1, :], modc(3))
            mixer(p, True, par[:, P_PM + p:P_PM + p + 1])
        for t in range(NT):
            load_x(t)
            norm_mod(der[:, 0, :], modc(0))
            ffn(0, der[:, 3, :])
            if stage >= 2:
                norm_mod(der[:, 1, :], modc(3))
                mixer(t, False)
            if stage >= 3:
                norm_mod(der[:, 2, :], modc(6))
                ffn(1, der[:, 4, :])
            final_out(t)
        ykeys = [("y", t, s) for t in range(NT) for s in range(4)]
        S.op("sp", None, ykeys, [])
        S.emit(nc, stack)
    return nc


def _lay(v):
    v = np.asarray(v, np.float32)
    return np.ascontiguousarray(v.reshape(-1, 128).T)


def make_in_maps(inp):
    f32 = np.float32
    cst = np.zeros((128, NCST), f32)
    cst[:, C_ID:C_ID + 128] = np.eye(128, dtype=f32)
    cst[:, C_MU:C_MU + 128] = np.triu(np.ones((128, 128), f32))
    cst[:, C_ONE:C_ONE + 128] = 1.0
    sm = np.ones(512, f32)
    sm[::CH] = 0.0
    cst[:, C_SC:C_SC + 512] = sm[None, :]
    shared = {
        "cst": cst,
        "ada_w": np.ascontiguousarray(inp["ada_w"][0], f32),
        "wi1": np.ascontiguousarray(inp["ffn1_wi"][0], f32),
        "wo1": np.ascontiguousarray(inp["ffn1_wo"][0], f32),
        "wi2": np.ascontiguousarray(inp["ffn2_wi"][0], f32),
        "wo2": np.ascontiguousarray(inp["ffn2_wo"][0], f32),
        "w_in": np.ascontiguousarray(inp["w_in"][0], f32),
        "w2": np.ascontiguousarray(inp["gla_gate_w2"][0], f32),
        "w_out": np.ascontiguousarray(inp["w_out"][0], f32),
    }
    maps = []
    for core in range(8):
        b, seg = core // 4, core % 4
        par = np.zeros((128, NPAR), f32)
        par[:, P_C:P_C + 16] = _lay(inp["c"][b])
        par[:, P_ADAB:P_ADAB + 144] = _lay(inp["ada_b"][0])
        par[:, P_NW1:P_NW1 + 16] = _lay(inp["norm_ffn1_w"][0])
        par[:, P_NW2:P_NW2 + 16] = _lay(inp["norm_mix_w"][0])
        par[:, P_NW3:P_NW3 + 16] = _lay(inp["norm_ffn2_w"][0])
        par[:, P_NWF:P_NWF + 16] = _lay(inp["final_norm_w"])
        par[:, P_B2:P_B2 + 4] = _lay(inp["gla_gate_b2"][0])
        par[:, P_GNW:P_GNW + 2] = _lay(inp["gla_norm_w"][0])
        par[:, P_HNW:P_HNW + 1] = _lay(inp["hgrn_norm_w"][0])
        par[:, P_LB0:P_LB0 + 8] = _lay(inp["hgrn_lower_bounds"][0])
        par[:, P_LB1:P_LB1 + 8] = _lay(inp["hgrn_lower_bounds"][1])
        xp = np.zeros((NPRE * TT, D), f32)
        for p in range(NPRE):
            t0 = seg * TCORE - NPRE * TT + p * TT
            if t0 >= 0:
                par[:, P_PM + p] = 1.0
                xp[p * TT:(p + 1) * TT] = inp["x"][b, t0:t0 + TT, :]
        m = dict(shared)
        m["par"] = par
        m["xp"] = xp
        m["x"] = np.ascontiguousarray(inp["x"][b, seg * TCORE:(seg + 1) * TCORE, :], f32)
        maps.append(m)
    return maps


_NC_CACHE = {}


def kernel(**inputs):
    stage = 99
    if stage not in _NC_CACHE:
        _NC_CACHE[stage] = build(stage)
    nc = _NC_CACHE[stage]
    maps = make_in_maps(inputs)
    res = run_bass_kernel_spmd(nc, maps, core_ids=list(range(8)))
    out = np.empty((2, 8192, D), np.float32)
    for core in range(8):
        b, seg = core // 4, core % 4
        out[b, seg * TCORE:(seg + 1) * TCORE, :] = res.results[core]["y"]
    return out
```

```python
import numpy as np
import concourse.bass as bass
import concourse.mybir as mybir
from concourse.bass_utils import run_bass_kernel_spmd
from contextlib import ExitStack

F32 = mybir.dt.float32
BF16 = mybir.dt.bfloat16
AF = mybir.ActivationFunctionType
ALU = mybir.AluOpType
AX = mybir.AxisListType

D = 2048
KC = 16
TCORE = 2048
TT = 512
NT = TCORE // TT
FF = 5504
FC = 43
INW = 7184
CH = 128
NCH = TT // CH
EPS = 1e-6
NSLOT = 3
SW = 2048

P_C = 0
P_ADAB = P_C + 16
P_NW1 = P_ADAB + 144
P_NW2 = P_NW1 + 16
P_NW3 = P_NW2 + 16
P_NWF = P_NW3 + 16
P_B2 = P_NWF + 16
P_GNW = P_B2 + 4
P_HNW = P_GNW + 2
P_LB0 = P_HNW + 1
P_LB1 = P_LB0 + 8
P_PM = P_LB1 + 8
NPRE = 12
NPAR = P_PM + NPRE
C_ID = 0
C_MU = 128
C_ONE = 256
C_SC = 384
NCST = 384 + 512


class Sched:
    def __init__(self):
        self.ops = []
        self.last_w = {}
        self.readers = {}

    def op(self, eng, fn, reads=(), writes=(), dma=None, nodep=()):
        oid = len(self.ops)
        deps = {}
        for k in reads:
            w = self.last_w.get(k)
            if w is not None:
                deps[w] = True
        for k in writes:
            w = self.last_w.get(k)
            if w is not None:
                deps[w] = True
            for r in self.readers.get(k, ()):
                if r not in deps:
                    deps[r] = False
        for k in reads:
            self.readers.setdefault(k, []).append(oid)
        for k in writes:
            self.last_w[k] = oid
            self.readers[k] = []
        deps.pop(oid, None)
        for n in nodep:
            deps.pop(n, None)
        self.ops.append(dict(eng=eng, fn=fn, deps=deps, dma=dma))
        return oid

    def emit(self, nc, stack):
        ops = self.ops
        signaled = set()
        for o in ops:
            signaled.update(o["deps"].keys())
        sems = {}

        def sem(key):
            if key not in sems:
                sems[key] = stack.enter_context(nc.semaphore("s%d" % len(sems)))
            return sems[key]

        cnt = {}
        for i, o in enumerate(ops):
            if o["dma"] is not None:
                k = ("dma", o["dma"])
                cnt[k] = cnt.get(k, 0) + 16
                o["tok"] = (k, cnt[k], 16)
            elif i in signaled:
                k = ("eng", o["eng"])
                cnt[k] = cnt.get(k, 0) + 1
                o["tok"] = (k, cnt[k], 1)
            else:
                o["tok"] = None
        for k in cnt:
            sem(k)
        per = {}
        for i, o in enumerate(ops):
            per.setdefault(o["eng"], []).append(i)
        block = stack.enter_context(nc.Block())

        def run(engname, e):
            seen = {}
            for i in per.get(engname, []):
                o = ops[i]
                for d in sorted(o["deps"].keys()):
                    od = ops[d]
                    flow = o["deps"][d]
                    if od["dma"] is None and od["eng"] == engname:
                        if engname == "pe" or not flow:
                            continue
                    k, v, _ = od["tok"]
                    if seen.get(k, 0) < v:
                        e.wait_ge(sems[k], v)
                        seen[k] = v
                if o["fn"] is None:
                    continue
                inst = o["fn"](e)
                if o["tok"] is not None:
                    k, v, amt = o["tok"]
                    inst.then_inc(sems[k], amt)

        @block.sync
        def _(e):
            run("sp", e)

        @block.scalar
        def _(e):
            run("act", e)

        @block.vector
        def _(e):
            run("dve", e)

        @block.gpsimd
        def _(e):
            run("pool", e)

        @block.tensor
        def _(e):
            run("pe", e)


def build(stage=99):
    nc = bass.Bass("TRN2", target_bir_lowering=False)

    def dram(name, shape, dt=F32, kind="ExternalInput"):
        return nc.dram_tensor(name, shape, dt, kind=kind).ap()

    x_d = dram("x", [TCORE, D])
    xp_d = dram("xp", [NPRE * TT, D])
    par_d = dram("par", [128, NPAR])
    cst_d = dram("cst", [128, NCST])
    adaw_d = dram("ada_w", [D, 9 * D])
    wi_d = [dram("wi1", [D, 2 * FF]), dram("wi2", [D, 2 * FF])]
    wo_d = [dram("wo1", [FF, D]), dram("wo2", [FF, D])]
    win_d = dram("w_in", [D, INW])
    w2_d = dram("w2", [16, 512])
    wout_d = dram("w_out", [D, D])
    y_d = dram("y", [TCORE, D], kind="ExternalOutput")

    NWC = 128
    wsc_d = dram("wsc", [NWC * 128, 8192], BF16, kind="Internal")

    adaw_v = adaw_d.rearrange("(kc p) n -> p kc n", p=128)
    wi_v = [w.rearrange("(kc p) (two f) -> p kc two f", p=128, two=2) for w in wi_d]
    wo_v = [w.rearrange("(fc p) d -> p fc d", p=128) for w in wo_d]
    win_v = win_d.rearrange("(kc p) n -> p kc n", p=128)
    wout_v = wout_d.rearrange("(kc p) n -> p kc n", p=128)

    S = Sched()
    stack = ExitStack()
    with stack:
        def sb(name, shape, dt=F32):
            return stack.enter_context(nc.sbuf_tensor("sb_" + name, shape, dt))

        def pst(name, shape, dt=F32):
            return stack.enter_context(nc.psum_tensor("pp_" + name, shape, dt))

        par = sb("par", [128, NPAR])
        cst = sb("cst", [128, NCST])
        identb = sb("identb", [128, 128], BF16)
        maskb = sb("maskb", [128, 128], BF16)
        onesb = sb("onesb", [128, 128], BF16)
        cact = sb("cact", [128, 16])
        mod = sb("mod", [128, 144])
        der = sb("der", [128, 5, 16])
        lbv = sb("lbv", [128, 3, 8])
        nb2 = sb("nb2", [128, 4])
        xT = sb("xT", [128, KC, TT])
        hT = sb("hT", [128, KC, TT], BF16)
        arena = sb("arena", [128, FC * TT], BF16)
        wsl = [sb("wsl%d" % i, [128, 8192], BF16) for i in range(NSLOT)]
        stg = [sb("stg%d" % i, [128, D]) for i in range(2)]
        rstd = sb("rstd", [128, TT])
        sqb = [sb("sqb%d" % i, [128, TT], BF16) for i in range(2)]
        grT = sb("grT", [16, TT], BF16)
        w2b = sb("w2b", [16, 512], BF16)
        wgr = sb("wgr", [128, KC, 16], BF16)
        cl = sb("cl", [128, TT])
        eq = sb("eq", [128, TT])
        ek = sb("ek", [128, TT])
        u1 = sb("u1", [128, TT])
        u2 = sb("u2", [128, TT])
        qt = sb("qt", [128, TT], BF16)
        kt = sb("kt", [128, TT], BF16)
        ktok = sb("ktok", [128, NCH, 128], BF16)
        PTall = sb("PTall", [128, TT], BF16)
        Sall = sb("Sall", [128, SW])
        Sbf2 = [sb("Sbf%d" % i, [128, 2048], BF16) for i in range(2)]
        Tst2 = [sb("Tst%d" % i, [128, 256]) for i in range(2)]
        sgb = [sb("sgb%d" % i, [128, TT]) for i in range(2)]
        tiny = sb("tiny", [128, 16])

        ps = [pst("ps%d" % i, [128, 512]) for i in range(7)]
        pstr = pst("pstr", [128, 1024], BF16)

        gT = arena[:, :].rearrange("p (f t) -> p f t", t=TT)
        vtok = arena[:, 0:8192].rearrange("p (s v) -> p s v", v=2048)
        ogT = arena[:, 8192:16384].rearrange("p (c t) -> p c t", t=TT)

        def akeys(lo, hi):
            return [("ar", i) for i in range(lo // 512, (hi + 511) // 512)]

        ident = cst[:, C_ID:C_ID + 128]
        scanm = cst[:, C_SC:C_SC + 512]

        def act(out, in_, func, reads, writes, bias=None, scale=None):
            kw = {}
            if bias is not None:
                kw["bias"] = bias
            if scale is not None:
                kw["scale"] = scale
            S.op("act", lambda e: e.activation(out=out, in_=in_, func=func, **kw), reads, writes)

        def tt(out, in0, in1, op, reads, writes, eng="dve"):
            S.op(eng, lambda e: e.tensor_tensor(out=out, in0=in0, in1=in1, op=op), reads, writes)

        def ts(out, in0, s1, s2, op0, op1, reads, writes, eng="dve"):
            if op1 is None:
                S.op(eng, lambda e: e.tensor_scalar(out=out, in0=in0, scalar1=s1, scalar2=None, op0=op0), reads, writes)
            else:
                S.op(eng, lambda e: e.tensor_scalar(out=out, in0=in0, scalar1=s1, scalar2=s2, op0=op0, op1=op1), reads, writes)

        def stt(out, in0, scalar, in1, op0, op1, reads, writes):
            S.op("dve", lambda e: e.scalar_tensor_tensor(out=out, in0=in0, scalar=scalar, in1=in1, op0=op0, op1=op1), reads, writes)

        def cp(out, in_, reads, writes, eng="dve"):
            if eng == "act":
                S.op("act", lambda e: e.copy(out=out, in_=in_), reads, writes)
            else:
                S.op(eng, lambda e: e.tensor_copy(out=out, in_=in_), reads, writes)

        def mmgroup(items, reads, writes):
            def fn(e):
                inst = None
                for (o, l, r, st, sp) in items:
                    inst = e.matmul(o, l, r, start=st, stop=sp)
                return inst
            S.op("pe", fn, reads, writes)

        def transposes(items, reads, writes):
            def fn(e):
                inst = None
                for (o, i, idn) in items:
                    inst = e.transpose(o, i, idn)
                return inst
            S.op("pe", fn, reads, writes)

        def dma(q, out, in_, reads, writes, tag, nodep=()):
            return S.op(q, lambda e: e.dma_start(out=out, in_=in_), reads, writes, dma=tag, nodep=nodep)

        wstate = dict(n=0)
        wcache = {}

        def wload(segs, ckey=None, used=8192):
            slot = wstate["n"] % NSLOT
            wstate["n"] += 1
            key = ("w", slot)
            if ckey is not None and ckey in wcache:
                idx = wcache[ckey]
                dma("sp", wsl[slot][:, 0:used], wsc_d[idx * 128:(idx + 1) * 128, 0:used], [("wc", idx)], [key], ("w", slot))
                return slot, key
            ids = []
            for (dstf, src) in segs:
                ids.append(dma("pool", dstf(wsl[slot]), src, [], [key], ("w", slot), nodep=ids))
            if ckey is not None:
                idx = len(wcache)
                assert idx < NWC
                wcache[ckey] = idx
                dma("sp", wsc_d[idx * 128:(idx + 1) * 128, 0:used], wsl[slot][:, 0:used], [key], [("wc", idx)], ("wcs", slot))
            return slot, key

        dma("sp", par[:, :], par_d[:, :], [], ["par"], "par")
        dma("sp", cst[:, :], cst_d[:, :], [], ["cst"], "cst")
        dma("pool", w2b[:, :], w2_d[:, :], [], ["w2b"], "w2b")
        dma("pool", wgr[:, :, :], win_v[:, :, 3072:3088], [], ["wgr"], "wgr")
        cp(identb[:, :], cst[:, C_ID:C_ID + 128], ["cst"], ["identb"])
        cp(maskb[:, :], cst[:, C_MU:C_MU + 128], ["cst"], ["maskb"])
        cp(onesb[:, :], cst[:, C_ONE:C_ONE + 128], ["cst"], ["onesb"])
        act(cact[:, :], par[:, P_C:P_C + 16], AF.Silu, ["par"], ["cact"])
        for j in range(144):
            slot = j % 2
            sv = stg[slot][:, :].rearrange("p (kc n) -> p kc n", n=128)
            dma("sp", sv, adaw_v[:, :, j * 128:(j + 1) * 128], [], [("stg", slot)], ("stg", slot))
            mmgroup([(ps[0][:, j:j + 1], sv[:, kc, :], cact[:, kc:kc + 1], kc == 0, kc == KC - 1) for kc in range(KC)],
                    [("stg", slot), "cact"], [("ps", 0)])
        tt(mod[:, :], ps[0][:, 0:144], par[:, P_ADAB:P_ADAB + 144], ALU.add, [("ps", 0), "par"], ["mod"])

        def modc(m):
            return mod[:, m * 16:(m + 1) * 16]
        for i, (pn, m) in enumerate([(P_NW1, 1), (P_NW2, 4), (P_NW3, 7)]):
            stt(der[:, i, :], modc(m), 1.0, par[:, pn:pn + 16], ALU.add, ALU.mult, ["mod", "par"], ["der"])
        ts(der[:, 3, :], modc(2), 0.5, None, ALU.mult, None, ["mod"], ["der"])
        ts(der[:, 4, :], modc(8), 0.5, None, ALU.mult, None, ["mod"], ["der"])
        tt(tiny[:, 0:8], par[:, P_LB0:P_LB0 + 8], par[:, P_LB1:P_LB1 + 8], ALU.subtract, ["par"], ["tiny"])
        act(lbv[:, 0, :], tiny[:, 0:8], AF.Sigmoid, ["tiny"], ["lbv"])
        ts(lbv[:, 1, :], lbv[:, 0, :], -1.0, 1.0, ALU.mult, ALU.add, ["lbv"], ["lbv"])
        ts(lbv[:, 2, :], lbv[:, 0, :], -1.0, None, ALU.add, None, ["lbv"], ["lbv"])
        ts(nb2[:, :], par[:, P_B2:P_B2 + 4], -1.0, None, ALU.mult, None, ["par"], ["nb2"])
        S.op("dve", lambda e: e.memset(Sall[:, :], 0.0), [], ["Sall"])
        S.op("dve", lambda e: e.memset(Sbf2[0][:, :], 0.0), [], [("Sb", h, 0) for h in range(12)])
        S.op("dve", lambda e: e.memset(Sbf2[1][:, :], 0.0), [], [("Sb", h, 1) for h in range(12)])

        def norm_mod(wm, sh):
            for c in range(KC):
                act(sqb[c % 2][:, :], xT[:, c, :], AF.Square, [("xT", c)], [("sqb", c % 2)])
                mmgroup([(ps[6][:, :], onesb[:, :], sqb[c % 2][:, :], c == 0, c == KC - 1)],
                        [("sqb", c % 2), "onesb"], [("ps", 6)])
            compute_rstd(ps[6], 1.0 / D, ("ps", 6), ps[6][:, :], ("ps", 6))
            for c in range(KC):
                t = u1 if c % 2 == 0 else u2
                tk = "u1" if c % 2 == 0 else "u2"
                tt(t[:, :], xT[:, c, :], ps[6][:, :], ALU.mult, [("xT", c), ("ps", 6)], [tk])
                act(hT[:, c, :], t[:, :], AF.Identity, [tk, "der", "mod"], ["hT"], bias=sh[:, c:c + 1], scale=wm[:, c:c + 1])

        def compute_rstd(psb, inv_n, pskey, out_ap=None, out_key="rstd"):
            out_ap = rstd[:, :] if out_ap is None else out_ap
            ts(rstd[:, :], psb[:, :], inv_n, EPS, ALU.mult, ALU.add, [pskey], ["rstd"])
            act(rstd[:, :], rstd[:, :], AF.Sqrt, ["rstd"], ["rstd"])
            S.op("dve", lambda e: e.reciprocal(out=out_ap, in_=rstd[:, :]), ["rstd"], [out_key])

        def ffn(fi, gate_half):
            ftiles = [(f0, 256) for f0 in range(0, 5376, 256)] + [(5376, 128)]
            pend = []

            def issue(k):
                f0, w = ftiles[k]
                pend.append(wload([(lambda s, w=w: s[:, 0:KC * w].rearrange("p (kc f) -> p kc f", f=w),
                                    wi_v[fi][:, :, 0, f0:f0 + w]),
                                   (lambda s, w=w: s[:, KC * w:2 * KC * w].rearrange("p (kc f) -> p kc f", f=w),
                                    wi_v[fi][:, :, 1, f0:f0 + w])], ckey=("wi", fi, k), used=2 * KC * w))
            issue(0)
            issue(1)
            cidx = 0
            for k, (f0, w) in enumerate(ftiles):
                if k + 2 < len(ftiles):
                    issue(k + 2)
                slot, key = pend[k]
                wva = wsl[slot][:, 0:KC * w].rearrange("p (kc f) -> p kc f", f=w)
                wvb = wsl[slot][:, KC * w:2 * KC * w].rearrange("p (kc f) -> p kc f", f=w)
                for j in range(w // 128):
                    f = f0 // 128 + j
                    pa = ps[(cidx % 2) * 2]
                    pb = ps[(cidx % 2) * 2 + 1]
                    ka = ("ps", (cidx % 2) * 2)
                    kb = ("ps", (cidx % 2) * 2 + 1)
                    cidx += 1
                    mmgroup([(pa[:, :], wva[:, kc, j * 128:(j + 1) * 128], hT[:, kc, :], kc == 0, kc == KC - 1) for kc in range(KC)],
                            [key, "hT"], [ka])
                    mmgroup([(pb[:, :], wvb[:, kc, j * 128:(j + 1) * 128], hT[:, kc, :], kc == 0, kc == KC - 1) for kc in range(KC)],
                            [key, "hT"], [kb])
                    t = u1 if f % 2 == 0 else u2
                    tk = "u1" if f % 2 == 0 else "u2"
                    act(t[:, :], pa[:, :], AF.Silu, [ka], [tk])
                    tt(gT[:, f, :], t[:, :], pb[:, :], ALU.mult, [tk, kb], [("ar", f)])
            pend2 = []

            def issue2(d):
                pend2.append(wload([(lambda s: s[:, 0:FC * 128].rearrange("p (fc n) -> p fc n", n=128),
                                     wo_v[fi][:, :, d * 128:(d + 1) * 128])], ckey=("wo", fi, d), used=FC * 128))
            issue2(0)
            issue2(1)
            for d in range(KC):
                if d + 2 < KC:
                    issue2(d + 2)
                slot, key = pend2[d]
                wv = wsl[slot][:, 0:FC * 128].rearrange("p (fc n) -> p fc n", n=128)
                po = ps[4 + d % 2]
                kp = ("ps", 4 + d % 2)
                mmgroup([(po[:, :], wv[:, f, :], gT[:, f, :], f == 0, f == FC - 1) for f in range(FC)],
                        [key] + [("ar", f) for f in range(FC)], [kp])
                stt(xT[:, d, :], po[:, :], gate_half[:, d:d + 1], xT[:, d, :], ALU.mult, ALU.add, [kp, ("xT", d), "der", "mod"], [("xT", d)])

        def load_x(t, src=None):
            src = x_d if src is None else src
            for s in range(4):
                slot = s % 2
                r0 = t * TT + s * 128
                dma("sp", stg[slot][:, :], src[r0:r0 + 128, :], [], [("stg", slot)], ("stg", slot))
                for g in range(4):
                    transposes([(ps[g][:, i * 128:(i + 1) * 128], stg[slot][:, (g * 4 + i) * 128:(g * 4 + i + 1) * 128], ident) for i in range(4)],
                               [("stg", slot), "cst"], [("ps", g)])
                    cp(xT[:, g * 4:(g + 1) * 4, s * 128:(s + 1) * 128],
                       ps[g][:, :].rearrange("p (c t) -> p c t", t=128), [("ps", g)], [("xT", g * 4 + i) for i in range(4)],
                       eng=("act" if g % 2 == 0 else "dve"))

        def final_out(t):
            for c in range(KC):
                act(sqb[c % 2][:, :], xT[:, c, :], AF.Square, [("xT", c)], [("sqb", c % 2)])
                mmgroup([(ps[6][:, :], onesb[:, :], sqb[c % 2][:, :], c == 0, c == KC - 1)],
                        [("sqb", c % 2), "onesb"], [("ps", 6)])
            compute_rstd(ps[6], 1.0 / D, ("ps", 6), ps[6][:, :], ("ps", 6))
            for c in range(KC):
                stt(xT[:, c, :], xT[:, c, :], par[:, P_NWF + c:P_NWF + c + 1], ps[6][:, :], ALU.mult, ALU.mult,
                    [("xT", c), "par", ("ps", 6)], [("xT", c)])
            for s in range(4):
                slot = s % 2
                for g in range(4):
                    transposes([(ps[g][:, i * 128:(i + 1) * 128], xT[:, g * 4 + i, s * 128:(s + 1) * 128], ident) for i in range(4)],
                               [("xT", g * 4 + i) for i in range(4)] + ["cst"], [("ps", g)])
                    cp(stg[slot][:, g * 512:(g + 1) * 512], ps[g][:, :], [("ps", g)], [("stg", slot)],
                       eng=("act" if g % 2 == 0 else "dve"))
                r0 = t * TT + s * 128
                dma("sp", y_d[r0:r0 + 128, :], stg[slot][:, :], [("stg", slot)], [("y", t, s)], ("stg", slot))

        def head_cols(h):
            if h < 4:
                return [(h * 128, 128), (512 + h * 128, 128), (2048 + h * 256, 256)]
            hh = h - 4
            return [(3088 + hh * 128, 128), (4112 + hh * 128, 128), (6160 + hh * 128, 128)]

        def head_geom(h):
            if h < 4:
                return 256, h * 256
            return 128, 1024 + (h - 4) * 128

        def mixer(t, pass1, mcol=None):
            mmgroup([(ps[2][0:16, :], wgr[:, kc, :], hT[:, kc, :], kc == 0, kc == KC - 1) for kc in range(KC)],
                    ["wgr", "hT"], [("ps", 2)])
            cp(grT[:, :], ps[2][0:16, :], [("ps", 2)], ["grT"])
            vgroups = [1024, 1536, 5136, 5648]
            pend = []

            def issue(i):
                c0 = vgroups[i]
                pend.append(wload([(lambda s: s[:, 0:KC * 512].rearrange("p (kc n) -> p kc n", n=512), win_v[:, :, c0:c0 + 512])], ckey=("v", i)))
            issue(0)
            issue(1)
            n = 0
            for i in range(4):
                if i + 2 < 4:
                    issue(i + 2)
                slot, key = pend[i]
                wv = wsl[slot][:, 0:KC * 512].rearrange("p (kc n) -> p kc n", n=512)
                for s in range(4):
                    pb = ps[n % 2]
                    kb = ("ps", n % 2)
                    mmgroup([(pb[:, :], hT[:, kc, s * 128:(s + 1) * 128], wv[:, kc, :], kc == 0, kc == KC - 1) for kc in range(KC)],
                            [key, "hT"], [kb])
                    lo = s * 2048 + i * 512
                    cp(vtok[:, s, i * 512:(i + 1) * 512], pb[:, :], [kb], akeys(lo, lo + 512), eng=("act" if n % 2 == 0 else "dve"))
                    n += 1
            hp = []

            def hissue(h):
                segs = []
                off = 0
                for si, (c0, w) in enumerate(head_cols(h)):
                    segs.append((lambda s, off=off, w=w: s[:, 0:KC * 512].rearrange("p (kc n) -> p kc n", n=512)[:, :, off:off + w],
                                 win_v[:, :, c0:c0 + w]))
                    off += w
                hp.append(wload(segs, ckey=("h2", h)))
            if pass1:
                kt1 = [wload([(lambda s: s[:, 0:KC * 512].rearrange("p (kc n) -> p kc n", n=512), win_v[:, :, c0:c0 + 512])], ckey=("k1", c0))
                       for c0 in (512, 4112, 4624)]
            else:
                hissue(0)
                hissue(1)
            for h in range(12):
                if pass1:
                    ti = 0 if h < 4 else 1 + (h - 4) // 4
                    slot, key = kt1[ti]
                    koff = (h % 4) * 128
                else:
                    if h + 2 < 12:
                        hissue(h + 2)
                    slot, key = hp[h]
                    koff = 128
                wv = wsl[slot][:, 0:KC * 512].rearrange("p (kc n) -> p kc n", n=512)
                dv, cb = head_geom(h)
                nj = dv // 128
                gla = h < 4
                Sh = Sall[:, cb:cb + dv]
                sk = ("S", h)
                if not pass1:
                    mmgroup([(ps[0][:, :], wv[:, kc, 0:128], hT[:, kc, :], kc == 0, kc == KC - 1) for kc in range(KC)],
                            [key, "hT"], [("ps", 0)])
                mmgroup([(ps[1][:, :], wv[:, kc, koff:koff + 128], hT[:, kc, :], kc == 0, kc == KC - 1) for kc in range(KC)],
                        [key, "hT"], [("ps", 1)])
                if gla:
                    mmgroup([(ps[2][:, :], w2b[:, h * 128:(h + 1) * 128], grT[:, :], True, True)], ["w2b", "grT"], [("ps", 2)])
                    act(u1[:, :], ps[2][:, :], AF.Exp, [("ps", 2), "nb2"], ["u1"], bias=nb2[:, h:h + 1], scale=-1.0)
                    act(u2[:, :], u1[:, :], AF.Ln, ["u1"], ["u2"], bias=1.0)
                    S.op("dve", lambda e: e.tensor_tensor_scan(out=cl[:, :], data0=scanm, data1=u2[:, :], initial=0.0, op0=ALU.mult, op1=ALU.add),
                         ["cst", "u2"], ["cl"])
                    qs, ks = -1.0 / 16.0, 1.0 / 16.0
                else:
                    hh = h - 4
                    act(u1[:, :], ps[1][:, :], AF.Sigmoid, [("ps", 1)], ["u1"])
                    act(u2[:, :], u1[:, :], AF.Ln, ["u1", "lbv"], ["u2"], bias=lbv[:, 0, hh:hh + 1], scale=lbv[:, 1, hh:hh + 1])
                    S.op("dve", lambda e: e.tensor_tensor_scan(out=cl[:, :], data0=scanm, data1=u2[:, :], initial=0.0, op0=ALU.mult, op1=ALU.add),
                         ["cst", "u2"], ["cl"])
                    qs, ks = 1.0, -1.0
                act(ek[:, :], cl[:, :], AF.Exp, ["cl"], ["ek"], scale=ks)
                act(eq[:, :], cl[:, :], AF.Exp, ["cl"], ["eq"], scale=qs)
                if gla:
                    if pass1:
                        stt(kt[:, :], ps[1][:, :], mcol, ek[:, :], ALU.mult, ALU.mult, [("ps", 1), "ek", "par"], ["kt"])
                    else:
                        tt(kt[:, :], ps[1][:, :], ek[:, :], ALU.mult, [("ps", 1), "ek"], ["kt"])
                    if not pass1:
                        stt(qt[:, :], ps[0][:, :], 128.0 ** -0.5, eq[:, :], ALU.mult, ALU.mult, [("ps", 0), "eq"], ["qt"])
                else:
                    ts(u1[:, :], u1[:, :], lbv[:, 2, hh:hh + 1], lbv[:, 1, hh:hh + 1], ALU.mult, ALU.add, ["u1", "lbv"], ["u1"])
                    if pass1:
                        stt(kt[:, :], u1[:, :], mcol, ek[:, :], ALU.mult, ALU.mult, ["u1", "ek", "par"], ["kt"])
                    else:
                        tt(kt[:, :], u1[:, :], ek[:, :], ALU.mult, ["u1", "ek"], ["kt"])
                    if not pass1:
                        act(u2[:, :], ps[0][:, :], AF.Silu, [("ps", 0)], ["u2"])
                        tt(qt[:, :], u2[:, :], eq[:, :], ALU.mult, ["u2", "eq"], ["qt"])
                if not pass1:
                    for j in range(nj):
                        mmgroup([(ps[5][:, :], wv[:, kc, 256 + j * 128:256 + (j + 1) * 128], hT[:, kc, :], kc == 0, kc == KC - 1) for kc in range(KC)],
                                [key, "hT"], [("ps", 5)])
                        act(sgb[j][:, :], ps[5][:, :], AF.Silu, [("ps", 5)], [("sgb", j)])
                transposes([(pstr[:, c * 128:(c + 1) * 128], kt[:, c * 128:(c + 1) * 128], identb[:, :]) for c in range(NCH)],
                           ["kt", "identb"], ["pstr"])
                cp(ktok[:, :, :], pstr[:, 0:NCH * 128].rearrange("p (c n) -> p c n", n=128), ["pstr"], ["ktok"], eng="act")
                po = [ps[3], ps[4]]
                pok = [("ps", 3), ("ps", 4)]

                def kvloc(c):
                    if gla:
                        return (ps[6], ("ps", 6), (c % 2) * 256) if c < 2 else (ps[2], ("ps", 2), (c % 2) * 256)
                    return (ps[6], ("ps", 6), c * 128)

                def vch_of(c):
                    return vtok[:, c, cb:cb + dv], akeys(c * 2048 + cb, c * 2048 + cb + dv)
                for c in range(NCH):
                    kb_, kk_, ko_ = kvloc(c)
                    vch, vkeys = vch_of(c)
                    mmgroup([(kb_[:, ko_:ko_ + dv], ktok[:, c, :], vch, True, True)], ["ktok"] + vkeys, [kk_])
                if not pass1:
                    for c in range(NCH):
                        csl = slice(c * CH, (c + 1) * CH)
                        mmgroup([(ps[5][:, csl], kt[:, csl], qt[:, csl], True, True)], ["kt", "qt"], [("ps", 5)])
                    for c in range(NCH):
                        csl = slice(c * CH, (c + 1) * CH)
                        tt(PTall[:, csl], ps[5][:, csl], maskb[:, :], ALU.mult, [("ps", 5), "maskb"], [("PT", c)])
                for c in range(NCH):
                    csl = slice(c * CH, (c + 1) * CH)
                    vch, vkeys = vch_of(c)
                    if not pass1:
                        Sbin = Sbf2[c % 2][:, cb:cb + dv]
                        for j in range(nj):
                            mmgroup([(po[j][:, csl], vch[:, j * 128:(j + 1) * 128], PTall[:, csl], True, False),
                                     (po[j][:, csl], Sbin[:, j * 128:(j + 1) * 128], qt[:, csl], False, True)],
                                    vkeys + [("PT", c), ("Sb", h, c % 2), "qt"], [pok[j]])
                    kb_, kk_, ko_ = kvloc(c)
                    T = Tst2[c % 2]
                    tk_ = ("Tst", c % 2)
                    a_ap = eq[:, c * CH + CH - 1:c * CH + CH]
                    tt(T[:, 0:dv], kb_[:, ko_:ko_ + dv], Sh, ALU.add, [kk_, sk], [tk_])
                    ts(Sh, T[:, 0:dv], a_ap, None, ALU.mult, None, [tk_, "eq"], [sk])
                    if not pass1:
                        act(Sbf2[(c + 1) % 2][:, cb:cb + dv], T[:, 0:dv], AF.Identity, [tk_, "eq"], [("Sb", h, (c + 1) % 2)], scale=a_ap)
                if pass1:
                    continue
                for j in range(nj):
                    act(sqb[j][:, :], po[j][:, :], AF.Square, [pok[j]], [("sqb", j)])
                    mmgroup([(ps[5][:, :], onesb[:, :], sqb[j][:, :], j == 0, j == nj - 1)], [("sqb", j), "onesb"], [("ps", 5)])
                compute_rstd(ps[5], 1.0 / dv, ("ps", 5))
                for j in range(nj):
                    tt(u2[:, :], po[j][:, :], rstd[:, :], ALU.mult, [pok[j], "rstd"], ["u2"])
                    nwc = par[:, P_GNW + j:P_GNW + j + 1] if gla else par[:, P_HNW:P_HNW + 1]
                    fc = cb // 128 + j
                    stt(ogT[:, fc, :], u2[:, :], nwc, sgb[j][:, :], ALU.mult, ALU.mult, ["u2", ("sgb", j), "par"], akeys(8192 + fc * 512, 8192 + fc * 512 + 512))
            if pass1:
                return
            pw = []

            def wissue(i):
                pw.append(wload([(lambda s: s[:, 0:KC * 512].rearrange("p (kc n) -> p kc n", n=512), wout_v[:, :, i * 512:(i + 1) * 512])], ckey=("wout", i)))
            wissue(0)
            wissue(1)
            for i in range(4):
                if i + 2 < 4:
                    wissue(i + 2)
                slot, key = pw[i]
                wv = wsl[slot][:, 0:KC * 512].rearrange("p (kc n) -> p kc n", n=512)
                for jj in range(4):
                    d = i * 4 + jj
                    pb = ps[d % 2]
                    kb = ("ps", d % 2)
                    mmgroup([(pb[:, :], wv[:, kc, jj * 128:(jj + 1) * 128], ogT[:, kc, :], kc == 0, kc == KC - 1) for kc in range(KC)],
                            [key] + akeys(8192, 16384), [kb])
                    stt(xT[:, d, :], pb[:, :], mod[:, 5 * 16 + d:5 * 16 + d + 1], xT[:, d, :], ALU.mult, ALU.add, [kb, ("xT", d), "mod"], [("xT", d)])

        npre = NPRE if stage >= 2 else 0
        for p in range(npre):
            load_x(p, xp_d)
            norm_mod(der[:, 0, :], modc(0))
            ffn(0, der[:, 3, :])
            norm_mod(der[:, 1, :], modc(3))
            mixer(p, True, par[:, P_PM + p:P_PM + p + 1])
        if stage >= 2:
            cp(Sbf2[0][:, :], Sall[:, 0:2048], [("S", h) for h in range(12)], [("Sb", h, 0) for h in range(12)])
        for t in range(NT):
            load_x(t)
            norm_mod(der[:, 0, :], modc(0))
            ffn(0, der[:, 3, :])
            if stage >= 2:
                norm_mod(der[:, 1, :], modc(3))
                mixer(t, False)
            if stage >= 3:
                norm_mod(der[:, 2, :], modc(6))
                ffn(1, der[:, 4, :])
            final_out(t)
        ykeys = [("y", t, s) for t in range(NT) for s in range(4)]
        S.op("sp", None, ykeys, [])
        S.emit(nc, stack)
    return nc


def _lay(v):
    v = np.asarray(v, np.float32)
    return np.ascontiguousarray(v.reshape(-1, 128).T)


def make_in_maps(inp):
    f32 = np.float32
    cst = np.zeros((128, NCST), f32)
    cst[:, C_ID:C_ID + 128] = np.eye(128, dtype=f32)
    cst[:, C_MU:C_MU + 128] = np.triu(np.ones((128, 128), f32))
    cst[:, C_ONE:C_ONE + 128] = 1.0
    sm = np.ones(512, f32)
    sm[::CH] = 0.0
    cst[:, C_SC:C_SC + 512] = sm[None, :]
    shared = {
        "cst": cst,
        "ada_w": np.ascontiguousarray(inp["ada_w"][0], f32),
        "wi1": np.ascontiguousarray(inp["ffn1_wi"][0], f32),
        "wo1": np.ascontiguousarray(inp["ffn1_wo"][0], f32),
        "wi2": np.ascontiguousarray(inp["ffn2_wi"][0], f32),
        "wo2": np.ascontiguousarray(inp["ffn2_wo"][0], f32),
        "w_in": np.ascontiguousarray(inp["w_in"][0], f32),
        "w2": np.ascontiguousarray(inp["gla_gate_w2"][0], f32),
        "w_out": np.ascontiguousarray(inp["w_out"][0], f32),
    }
    maps = []
    for core in range(8):
        b, seg = core // 4, core % 4
        par = np.zeros((128, NPAR), f32)
        par[:, P_C:P_C + 16] = _lay(inp["c"][b])
        par[:, P_ADAB:P_ADAB + 144] = _lay(inp["ada_b"][0])
        par[:, P_NW1:P_NW1 + 16] = _lay(inp["norm_ffn1_w"][0])
        par[:, P_NW2:P_NW2 + 16] = _lay(inp["norm_mix_w"][0])
        par[:, P_NW3:P_NW3 + 16] = _lay(inp["norm_ffn2_w"][0])
        par[:, P_NWF:P_NWF + 16] = _lay(inp["final_norm_w"])
        par[:, P_B2:P_B2 + 4] = _lay(inp["gla_gate_b2"][0])
        par[:, P_GNW:P_GNW + 2] = _lay(inp["gla_norm_w"][0])
        par[:, P_HNW:P_HNW + 1] = _lay(inp["hgrn_norm_w"][0])
        par[:, P_LB0:P_LB0 + 8] = _lay(inp["hgrn_lower_bounds"][0])
        par[:, P_LB1:P_LB1 + 8] = _lay(inp["hgrn_lower_bounds"][1])
        xp = np.zeros((NPRE * TT, D), f32)
        for p in range(NPRE):
            t0 = seg * TCORE - NPRE * TT + p * TT
            if t0 >= 0:
                par[:, P_PM + p] = 1.0
                xp[p * TT:(p + 1) * TT] = inp["x"][b, t0:t0 + TT, :]
        m = dict(shared)
        m["par"] = par
        m["xp"] = xp
        m["x"] = np.ascontiguousarray(inp["x"][b, seg * TCORE:(seg + 1) * TCORE, :], f32)
        maps.append(m)
    return maps


_NC_CACHE = {}


def kernel(**inputs):
    stage = 99
    if stage not in _NC_CACHE:
        _NC_CACHE[stage] = build(stage)
    nc = _NC_CACHE[stage]
    maps = make_in_maps(inputs)
    res = run_bass_kernel_spmd(nc, maps, core_ids=list(range(8)))
    out = np.empty((2, 8192, D), np.float32)
    for core in range(8):
        b, seg = core // 4, core % 4
        out[b, seg * TCORE:(seg + 1) * TCORE, :] = res.results[core]["y"]
    return out
```

```python
import numpy as np
import concourse.bass as bass
import concourse.mybir as mybir
from concourse.bass_utils import run_bass_kernel_spmd
from contextlib import ExitStack

F32 = mybir.dt.float32
BF16 = mybir.dt.bfloat16
AF = mybir.ActivationFunctionType
ALU = mybir.AluOpType
AX = mybir.AxisListType

D = 2048
KC = 16
TCORE = 2048
TT = 512
NT = TCORE // TT
FF = 5504
FC = 43
INW = 7184
CH = 128
NCH = TT // CH
EPS = 1e-6
NSLOT = 3
SW = 2048

P_C = 0
P_ADAB = P_C + 16
P_NW1 = P_ADAB + 144
P_NW2 = P_NW1 + 16
P_NW3 = P_NW2 + 16
P_NWF = P_NW3 + 16
P_B2 = P_NWF + 16
P_GNW = P_B2 + 4
P_HNW = P_GNW + 2
P_LB0 = P_HNW + 1
P_LB1 = P_LB0 + 8
P_PM = P_LB1 + 8
NPRE = 12
NPAR = P_PM + NPRE
C_ID = 0
C_MU = 128
C_ONE = 256
C_SC = 384
NCST = 384 + 512


class Sched:
    def __init__(self):
        self.ops = []
        self.last_w = {}
        self.readers = {}

    def op(self, eng, fn, reads=(), writes=(), dma=None, nodep=()):
        oid = len(self.ops)
        deps = {}
        for k in reads:
            w = self.last_w.get(k)
            if w is not None:
                deps[w] = True
        for k in writes:
            w = self.last_w.get(k)
            if w is not None:
                deps[w] = True
            for r in self.readers.get(k, ()):
                if r not in deps:
                    deps[r] = False
        for k in reads:
            self.readers.setdefault(k, []).append(oid)
        for k in writes:
            self.last_w[k] = oid
            self.readers[k] = []
        deps.pop(oid, None)
        for n in nodep:
            deps.pop(n, None)
        self.ops.append(dict(eng=eng, fn=fn, deps=deps, dma=dma))
        return oid

    def emit(self, nc, stack):
        ops = self.ops
        signaled = set()
        for o in ops:
            signaled.update(o["deps"].keys())
        sems = {}

        def sem(key):
            if key not in sems:
                sems[key] = stack.enter_context(nc.semaphore("s%d" % len(sems)))
            return sems[key]

        cnt = {}
        for i, o in enumerate(ops):
            if o["dma"] is not None:
                k = ("dma", o["dma"])
                cnt[k] = cnt.get(k, 0) + 16
                o["tok"] = (k, cnt[k], 16)
            elif i in signaled:
                k = ("eng", o["eng"])
                cnt[k] = cnt.get(k, 0) + 1
                o["tok"] = (k, cnt[k], 1)
            else:
                o["tok"] = None
        for k in cnt:
            sem(k)
        per = {}
        for i, o in enumerate(ops):
            per.setdefault(o["eng"], []).append(i)
        block = stack.enter_context(nc.Block())

        def run(engname, e):
            seen = {}
            for i in per.get(engname, []):
                o = ops[i]
                for d in sorted(o["deps"].keys()):
                    od = ops[d]
                    flow = o["deps"][d]
                    if od["dma"] is None and od["eng"] == engname:
                        if engname == "pe" or not flow:
                            continue
                    k, v, _ = od["tok"]
                    if seen.get(k, 0) < v:
                        e.wait_ge(sems[k], v)
                        seen[k] = v
                if o["fn"] is None:
                    continue
                inst = o["fn"](e)
                if o["tok"] is not None:
                    k, v, amt = o["tok"]
                    inst.then_inc(sems[k], amt)

        @block.sync
        def _(e):
            run("sp", e)

        @block.scalar
        def _(e):
            run("act", e)

        @block.vector
        def _(e):
            run("dve", e)

        @block.gpsimd
        def _(e):
            run("pool", e)

        @block.tensor
        def _(e):
            run("pe", e)


def build(stage=99):
    nc = bass.Bass("TRN2", target_bir_lowering=False)

    def dram(name, shape, dt=F32, kind="ExternalInput"):
        return nc.dram_tensor(name, shape, dt, kind=kind).ap()

    x_d = dram("x", [TCORE, D])
    xp_d = dram("xp", [NPRE * TT, D])
    par_d = dram("par", [128, NPAR])
    cst_d = dram("cst", [128, NCST])
    adaw_d = dram("ada_w", [D, 9 * D])
    wi_d = [dram("wi1", [D, 2 * FF]), dram("wi2", [D, 2 * FF])]
    wo_d = [dram("wo1", [FF, D]), dram("wo2", [FF, D])]
    win_d = dram("w_in", [D, INW])
    w2_d = dram("w2", [16, 512])
    wout_d = dram("w_out", [D, D])
    y_d = dram("y", [TCORE, D], kind="ExternalOutput")

    NWC = 128
    wsc_d = dram("wsc", [NWC * 128, 8192], BF16, kind="Internal")

    adaw_v = adaw_d.rearrange("(kc p) n -> p kc n", p=128)
    wi_v = [w.rearrange("(kc p) (two f) -> p kc two f", p=128, two=2) for w in wi_d]
    wo_v = [w.rearrange("(fc p) d -> p fc d", p=128) for w in wo_d]
    win_v = win_d.rearrange("(kc p) n -> p kc n", p=128)
    wout_v = wout_d.rearrange("(kc p) n -> p kc n", p=128)

    S = Sched()
    stack = ExitStack()
    with stack:
        def sb(name, shape, dt=F32):
            return stack.enter_context(nc.sbuf_tensor("sb_" + name, shape, dt))

        def pst(name, shape, dt=F32):
            return stack.enter_context(nc.psum_tensor("pp_" + name, shape, dt))

        par = sb("par", [128, NPAR])
        cst = sb("cst", [128, NCST])
        identb = sb("identb", [128, 128], BF16)
        maskb = sb("maskb", [128, 128], BF16)
        onesb = sb("onesb", [128, 128], BF16)
        cact = sb("cact", [128, 16])
        mod = sb("mod", [128, 144])
        der = sb("der", [128, 5, 16])
        lbv = sb("lbv", [128, 3, 8])
        nb2 = sb("nb2", [128, 4])
        xT = sb("xT", [128, KC, TT])
        hT = sb("hT", [128, KC, TT], BF16)
        arena = sb("arena", [128, FC * TT], BF16)
        wsl = [sb("wsl%d" % i, [128, 8192], BF16) for i in range(NSLOT)]
        stg = [sb("stg%d" % i, [128, D]) for i in range(2)]
        rstd = sb("rstd", [128, TT])
        sqb = [sb("sqb%d" % i, [128, TT], BF16) for i in range(2)]
        grT = sb("grT", [16, TT], BF16)
        w2b = sb("w2b", [16, 512], BF16)
        wgr = sb("wgr", [128, KC, 16], BF16)
        cl = sb("cl", [128, TT])
        eq = sb("eq", [128, TT])
        ek = sb("ek", [128, TT])
        u1 = sb("u1", [128, TT])
        u2 = sb("u2", [128, TT])
        qt = sb("qt", [128, TT], BF16)
        kt = sb("kt", [128, TT], BF16)
        ktok = sb("ktok", [128, NCH, 128], BF16)
        PTall = sb("PTall", [128, TT], BF16)
        Sall = sb("Sall", [128, SW])
        Sbf2 = [sb("Sbf%d" % i, [128, 2048], BF16) for i in range(2)]
        Tst2 = [sb("Tst%d" % i, [128, 256]) for i in range(2)]
        sgb = [sb("sgb%d" % i, [128, TT]) for i in range(2)]
        tiny = sb("tiny", [128, 16])

        ps = [pst("ps%d" % i, [128, 512]) for i in range(7)]
        pstr = pst("pstr", [128, 1024], BF16)

        gT = arena[:, :].rearrange("p (f t) -> p f t", t=TT)
        vtok = arena[:, 0:8192].rearrange("p (s v) -> p s v", v=2048)
        ogT = arena[:, 8192:16384].rearrange("p (c t) -> p c t", t=TT)

        def akeys(lo, hi):
            return [("ar", i) for i in range(lo // 512, (hi + 511) // 512)]

        ident = cst[:, C_ID:C_ID + 128]
        scanm = cst[:, C_SC:C_SC + 512]

        def act(out, in_, func, reads, writes, bias=None, scale=None):
            kw = {}
            if bias is not None:
                kw["bias"] = bias
            if scale is not None:
                kw["scale"] = scale
            S.op("act", lambda e: e.activation(out=out, in_=in_, func=func, **kw), reads, writes)

        def tt(out, in0, in1, op, reads, writes, eng="dve"):
            S.op(eng, lambda e: e.tensor_tensor(out=out, in0=in0, in1=in1, op=op), reads, writes)

        def ts(out, in0, s1, s2, op0, op1, reads, writes, eng="dve"):
            if op1 is None:
                S.op(eng, lambda e: e.tensor_scalar(out=out, in0=in0, scalar1=s1, scalar2=None, op0=op0), reads, writes)
            else:
                S.op(eng, lambda e: e.tensor_scalar(out=out, in0=in0, scalar1=s1, scalar2=s2, op0=op0, op1=op1), reads, writes)

        def stt(out, in0, scalar, in1, op0, op1, reads, writes):
            S.op("dve", lambda e: e.scalar_tensor_tensor(out=out, in0=in0, scalar=scalar, in1=in1, op0=op0, op1=op1), reads, writes)

        def cp(out, in_, reads, writes, eng="dve"):
            if eng == "act":
                S.op("act", lambda e: e.copy(out=out, in_=in_), reads, writes)
            else:
                S.op(eng, lambda e: e.tensor_copy(out=out, in_=in_), reads, writes)

        def mmgroup(items, reads, writes):
            def fn(e):
                inst = None
                for (o, l, r, st, sp) in items:
                    inst = e.matmul(o, l, r, start=st, stop=sp)
                return inst
            S.op("pe", fn, reads, writes)

        def transposes(items, reads, writes):
            def fn(e):
                inst = None
                for (o, i, idn) in items:
                    inst = e.transpose(o, i, idn)
                return inst
            S.op("pe", fn, reads, writes)

        def dma(q, out, in_, reads, writes, tag, nodep=()):
            return S.op(q, lambda e: e.dma_start(out=out, in_=in_), reads, writes, dma=tag, nodep=nodep)

        wstate = dict(n=0)
        wcache = {}

        def wload(segs, ckey=None, used=8192):
            slot = wstate["n"] % NSLOT
            wstate["n"] += 1
            key = ("w", slot)
            if ckey is not None and ckey in wcache:
                idx = wcache[ckey]
                dma("sp", wsl[slot][:, 0:used], wsc_d[idx * 128:(idx + 1) * 128, 0:used], [("wc", idx)], [key], ("wl", slot))
                return slot, key
            ids = []
            for (dstf, src) in segs:
                ids.append(dma("pool", dstf(wsl[slot]), src, [], [key], ("w", slot), nodep=ids))
            if ckey is not None:
                idx = len(wcache)
                assert idx < NWC
                wcache[ckey] = idx
                dma("sp", wsc_d[idx * 128:(idx + 1) * 128, 0:used], wsl[slot][:, 0:used], [key], [("wc", idx)], ("wcs", slot))
            return slot, key

        dma("sp", par[:, :], par_d[:, :], [], ["par"], "par")
        dma("sp", cst[:, :], cst_d[:, :], [], ["cst"], "cst")
        dma("pool", w2b[:, :], w2_d[:, :], [], ["w2b"], "w2b")
        dma("pool", wgr[:, :, :], win_v[:, :, 3072:3088], [], ["wgr"], "wgr")
        cp(identb[:, :], cst[:, C_ID:C_ID + 128], ["cst"], ["identb"])
        cp(maskb[:, :], cst[:, C_MU:C_MU + 128], ["cst"], ["maskb"])
        cp(onesb[:, :], cst[:, C_ONE:C_ONE + 128], ["cst"], ["onesb"])
        act(cact[:, :], par[:, P_C:P_C + 16], AF.Silu, ["par"], ["cact"])
        for j in range(144):
            slot = j % 2
            sv = stg[slot][:, :].rearrange("p (kc n) -> p kc n", n=128)
            dma("sp", sv, adaw_v[:, :, j * 128:(j + 1) * 128], [], [("stg", slot)], ("stg", slot))
            mmgroup([(ps[0][:, j:j + 1], sv[:, kc, :], cact[:, kc:kc + 1], kc == 0, kc == KC - 1) for kc in range(KC)],
                    [("stg", slot), "cact"], [("ps", 0)])
        tt(mod[:, :], ps[0][:, 0:144], par[:, P_ADAB:P_ADAB + 144], ALU.add, [("ps", 0), "par"], ["mod"])

        def modc(m):
            return mod[:, m * 16:(m + 1) * 16]
        for i, (pn, m) in enumerate([(P_NW1, 1), (P_NW2, 4), (P_NW3, 7)]):
            stt(der[:, i, :], modc(m), 1.0, par[:, pn:pn + 16], ALU.add, ALU.mult, ["mod", "par"], ["der"])
        ts(der[:, 3, :], modc(2), 0.5, None, ALU.mult, None, ["mod"], ["der"])
        ts(der[:, 4, :], modc(8), 0.5, None, ALU.mult, None, ["mod"], ["der"])
        tt(tiny[:, 0:8], par[:, P_LB0:P_LB0 + 8], par[:, P_LB1:P_LB1 + 8], ALU.subtract, ["par"], ["tiny"])
        act(lbv[:, 0, :], tiny[:, 0:8], AF.Sigmoid, ["tiny"], ["lbv"])
        ts(lbv[:, 1, :], lbv[:, 0, :], -1.0, 1.0, ALU.mult, ALU.add, ["lbv"], ["lbv"])
        ts(lbv[:, 2, :], lbv[:, 0, :], -1.0, None, ALU.add, None, ["lbv"], ["lbv"])
        ts(nb2[:, :], par[:, P_B2:P_B2 + 4], -1.0, None, ALU.mult, None, ["par"], ["nb2"])
        S.op("dve", lambda e: e.memset(Sall[:, :], 0.0), [], ["Sall"])
        S.op("dve", lambda e: e.memset(Sbf2[0][:, :], 0.0), [], [("Sb", h, 0) for h in range(12)])
        S.op("dve", lambda e: e.memset(Sbf2[1][:, :], 0.0), [], [("Sb", h, 1) for h in range(12)])

        def norm_mod(wm, sh):
            for c in range(KC):
                act(sqb[c % 2][:, :], xT[:, c, :], AF.Square, [("xT", c)], [("sqb", c % 2)])
                mmgroup([(ps[6][:, :], onesb[:, :], sqb[c % 2][:, :], c == 0, c == KC - 1)],
                        [("sqb", c % 2), "onesb"], [("ps", 6)])
            compute_rstd(ps[6], 1.0 / D, ("ps", 6), ps[6][:, :], ("ps", 6))
            for c in range(KC):
                t = u1 if c % 2 == 0 else u2
                tk = "u1" if c % 2 == 0 else "u2"
                tt(t[:, :], xT[:, c, :], ps[6][:, :], ALU.mult, [("xT", c), ("ps", 6)], [tk])
                act(hT[:, c, :], t[:, :], AF.Identity, [tk, "der", "mod"], ["hT"], bias=sh[:, c:c + 1], scale=wm[:, c:c + 1])

        def compute_rstd(psb, inv_n, pskey, out_ap=None, out_key="rstd"):
            out_ap = rstd[:, :] if out_ap is None else out_ap
            ts(rstd[:, :], psb[:, :], inv_n, EPS, ALU.mult, ALU.add, [pskey], ["rstd"])
            act(rstd[:, :], rstd[:, :], AF.Sqrt, ["rstd"], ["rstd"])
            S.op("dve", lambda e: e.reciprocal(out=out_ap, in_=rstd[:, :]), ["rstd"], [out_key])

        def ffn(fi, gate_half):
            ftiles = [(f0, 256) for f0 in range(0, 5376, 256)] + [(5376, 128)]
            pend = []

            def issue(k):
                f0, w = ftiles[k]
                pend.append(wload([(lambda s, w=w: s[:, 0:KC * w].rearrange("p (kc f) -> p kc f", f=w),
                                    wi_v[fi][:, :, 0, f0:f0 + w]),
                                   (lambda s, w=w: s[:, KC * w:2 * KC * w].rearrange("p (kc f) -> p kc f", f=w),
                                    wi_v[fi][:, :, 1, f0:f0 + w])], ckey=("wi", fi, k), used=2 * KC * w))
            issue(0)
            issue(1)
            cidx = 0
            for k, (f0, w) in enumerate(ftiles):
                if k + 2 < len(ftiles):
                    issue(k + 2)
                slot, key = pend[k]
                wva = wsl[slot][:, 0:KC * w].rearrange("p (kc f) -> p kc f", f=w)
                wvb = wsl[slot][:, KC * w:2 * KC * w].rearrange("p (kc f) -> p kc f", f=w)
                for j in range(w // 128):
                    f = f0 // 128 + j
                    pa = ps[(cidx % 2) * 2]
                    pb = ps[(cidx % 2) * 2 + 1]
                    ka = ("ps", (cidx % 2) * 2)
                    kb = ("ps", (cidx % 2) * 2 + 1)
                    cidx += 1
                    mmgroup([(pa[:, :], wva[:, kc, j * 128:(j + 1) * 128], hT[:, kc, :], kc == 0, kc == KC - 1) for kc in range(KC)],
                            [key, "hT"], [ka])
                    mmgroup([(pb[:, :], wvb[:, kc, j * 128:(j + 1) * 128], hT[:, kc, :], kc == 0, kc == KC - 1) for kc in range(KC)],
                            [key, "hT"], [kb])
                    t = u1 if f % 2 == 0 else u2
                    tk = "u1" if f % 2 == 0 else "u2"
                    act(t[:, :], pa[:, :], AF.Silu, [ka], [tk])
                    tt(gT[:, f, :], t[:, :], pb[:, :], ALU.mult, [tk, kb], [("ar", f)])
            pend2 = []

            def issue2(d):
                pend2.append(wload([(lambda s: s[:, 0:FC * 128].rearrange("p (fc n) -> p fc n", n=128),
                                     wo_v[fi][:, :, d * 128:(d + 1) * 128])], ckey=("wo", fi, d), used=FC * 128))
            issue2(0)
            issue2(1)
            for d in range(KC):
                if d + 2 < KC:
                    issue2(d + 2)
                slot, key = pend2[d]
                wv = wsl[slot][:, 0:FC * 128].rearrange("p (fc n) -> p fc n", n=128)
                po = ps[4 + d % 2]
                kp = ("ps", 4 + d % 2)
                mmgroup([(po[:, :], wv[:, f, :], gT[:, f, :], f == 0, f == FC - 1) for f in range(FC)],
                        [key] + [("ar", f) for f in range(FC)], [kp])
                stt(xT[:, d, :], po[:, :], gate_half[:, d:d + 1], xT[:, d, :], ALU.mult, ALU.add, [kp, ("xT", d), "der", "mod"], [("xT", d)])

        def load_x(t, src=None):
            src = x_d if src is None else src
            for s in range(4):
                slot = s % 2
                r0 = t * TT + s * 128
                dma("sp", stg[slot][:, :], src[r0:r0 + 128, :], [], [("stg", slot)], ("stg", slot))
                for g in range(4):
                    transposes([(ps[g][:, i * 128:(i + 1) * 128], stg[slot][:, (g * 4 + i) * 128:(g * 4 + i + 1) * 128], ident) for i in range(4)],
                               [("stg", slot), "cst"], [("ps", g)])
                    cp(xT[:, g * 4:(g + 1) * 4, s * 128:(s + 1) * 128],
                       ps[g][:, :].rearrange("p (c t) -> p c t", t=128), [("ps", g)], [("xT", g * 4 + i) for i in range(4)],
                       eng=("act" if g % 2 == 0 else "dve"))

        def final_out(t):
            for c in range(KC):
                act(sqb[c % 2][:, :], xT[:, c, :], AF.Square, [("xT", c)], [("sqb", c % 2)])
                mmgroup([(ps[6][:, :], onesb[:, :], sqb[c % 2][:, :], c == 0, c == KC - 1)],
                        [("sqb", c % 2), "onesb"], [("ps", 6)])
            compute_rstd(ps[6], 1.0 / D, ("ps", 6), ps[6][:, :], ("ps", 6))
            for c in range(KC):
                stt(xT[:, c, :], xT[:, c, :], par[:, P_NWF + c:P_NWF + c + 1], ps[6][:, :], ALU.mult, ALU.mult,
                    [("xT", c), "par", ("ps", 6)], [("xT", c)])
            for s in range(4):
                slot = s % 2
                for g in range(4):
                    transposes([(ps[g][:, i * 128:(i + 1) * 128], xT[:, g * 4 + i, s * 128:(s + 1) * 128], ident) for i in range(4)],
                               [("xT", g * 4 + i) for i in range(4)] + ["cst"], [("ps", g)])
                    cp(stg[slot][:, g * 512:(g + 1) * 512], ps[g][:, :], [("ps", g)], [("stg", slot)],
                       eng=("act" if g % 2 == 0 else "dve"))
                r0 = t * TT + s * 128
                dma("sp", y_d[r0:r0 + 128, :], stg[slot][:, :], [("stg", slot)], [("y", t, s)], ("stg", slot))

        def head_cols(h):
            if h < 4:
                return [(h * 128, 128), (512 + h * 128, 128), (2048 + h * 256, 256)]
            hh = h - 4
            return [(3088 + hh * 128, 128), (4112 + hh * 128, 128), (6160 + hh * 128, 128)]

        def head_geom(h):
            if h < 4:
                return 256, h * 256
            return 128, 1024 + (h - 4) * 128

        def mixer(t, pass1, mcol=None):
            mmgroup([(ps[2][0:16, :], wgr[:, kc, :], hT[:, kc, :], kc == 0, kc == KC - 1) for kc in range(KC)],
                    ["wgr", "hT"], [("ps", 2)])
            cp(grT[:, :], ps[2][0:16, :], [("ps", 2)], ["grT"])
            vgroups = [1024, 1536, 5136, 5648]
            pend = []

            def issue(i):
                c0 = vgroups[i]
                pend.append(wload([(lambda s: s[:, 0:KC * 512].rearrange("p (kc n) -> p kc n", n=512), win_v[:, :, c0:c0 + 512])], ckey=("v", i)))
            issue(0)
            issue(1)
            n = 0
            for i in range(4):
                if i + 2 < 4:
                    issue(i + 2)
                slot, key = pend[i]
                wv = wsl[slot][:, 0:KC * 512].rearrange("p (kc n) -> p kc n", n=512)
                for s in range(4):
                    pb = ps[n % 2]
                    kb = ("ps", n % 2)
                    mmgroup([(pb[:, :], hT[:, kc, s * 128:(s + 1) * 128], wv[:, kc, :], kc == 0, kc == KC - 1) for kc in range(KC)],
                            [key, "hT"], [kb])
                    lo = s * 2048 + i * 512
                    cp(vtok[:, s, i * 512:(i + 1) * 512], pb[:, :], [kb], akeys(lo, lo + 512), eng=("act" if n % 2 == 0 else "dve"))
                    n += 1
            hp = []

            def hissue(h):
                segs = []
                off = 0
                for si, (c0, w) in enumerate(head_cols(h)):
                    segs.append((lambda s, off=off, w=w: s[:, 0:KC * 512].rearrange("p (kc n) -> p kc n", n=512)[:, :, off:off + w],
                                 win_v[:, :, c0:c0 + w]))
                    off += w
                hp.append(wload(segs, ckey=("h2", h)))
            if pass1:
                kt1 = [wload([(lambda s: s[:, 0:KC * 512].rearrange("p (kc n) -> p kc n", n=512), win_v[:, :, c0:c0 + 512])], ckey=("k1", c0))
                       for c0 in (512, 4112, 4624)]
            else:
                hissue(0)
                hissue(1)
            for h in range(12):
                if pass1:
                    ti = 0 if h < 4 else 1 + (h - 4) // 4
                    slot, key = kt1[ti]
                    koff = (h % 4) * 128
                else:
                    if h + 2 < 12:
                        hissue(h + 2)
                    slot, key = hp[h]
                    koff = 128
                wv = wsl[slot][:, 0:KC * 512].rearrange("p (kc n) -> p kc n", n=512)
                dv, cb = head_geom(h)
                nj = dv // 128
                gla = h < 4
                Sh = Sall[:, cb:cb + dv]
                sk = ("S", h)
                if not pass1:
                    mmgroup([(ps[0][:, :], wv[:, kc, 0:128], hT[:, kc, :], kc == 0, kc == KC - 1) for kc in range(KC)],
                            [key, "hT"], [("ps", 0)])
                mmgroup([(ps[1][:, :], wv[:, kc, koff:koff + 128], hT[:, kc, :], kc == 0, kc == KC - 1) for kc in range(KC)],
                        [key, "hT"], [("ps", 1)])
                if gla:
                    mmgroup([(ps[2][:, :], w2b[:, h * 128:(h + 1) * 128], grT[:, :], True, True)], ["w2b", "grT"], [("ps", 2)])
                    act(u1[:, :], ps[2][:, :], AF.Exp, [("ps", 2), "nb2"], ["u1"], bias=nb2[:, h:h + 1], scale=-1.0)
                    act(u2[:, :], u1[:, :], AF.Ln, ["u1"], ["u2"], bias=1.0)
                    S.op("dve", lambda e: e.tensor_tensor_scan(out=cl[:, :], data0=scanm, data1=u2[:, :], initial=0.0, op0=ALU.mult, op1=ALU.add),
                         ["cst", "u2"], ["cl"])
                    qs, ks = -1.0 / 16.0, 1.0 / 16.0
                else:
                    hh = h - 4
                    act(u1[:, :], ps[1][:, :], AF.Sigmoid, [("ps", 1)], ["u1"])
                    act(u2[:, :], u1[:, :], AF.Ln, ["u1", "lbv"], ["u2"], bias=lbv[:, 0, hh:hh + 1], scale=lbv[:, 1, hh:hh + 1])
                    S.op("dve", lambda e: e.tensor_tensor_scan(out=cl[:, :], data0=scanm, data1=u2[:, :], initial=0.0, op0=ALU.mult, op1=ALU.add),
                         ["cst", "u2"], ["cl"])
                    qs, ks = 1.0, -1.0
                act(ek[:, :], cl[:, :], AF.Exp, ["cl"], ["ek"], scale=ks)
                act(eq[:, :], cl[:, :], AF.Exp, ["cl"], ["eq"], scale=qs)
                if gla:
                    if pass1:
                        stt(kt[:, :], ps[1][:, :], mcol, ek[:, :], ALU.mult, ALU.mult, [("ps", 1), "ek", "par"], ["kt"])
                    else:
                        tt(kt[:, :], ps[1][:, :], ek[:, :], ALU.mult, [("ps", 1), "ek"], ["kt"])
                    if not pass1:
                        stt(qt[:, :], ps[0][:, :], 128.0 ** -0.5, eq[:, :], ALU.mult, ALU.mult, [("ps", 0), "eq"], ["qt"])
                else:
                    ts(u1[:, :], u1[:, :], lbv[:, 2, hh:hh + 1], lbv[:, 1, hh:hh + 1], ALU.mult, ALU.add, ["u1", "lbv"], ["u1"])
                    if pass1:
                        stt(kt[:, :], u1[:, :], mcol, ek[:, :], ALU.mult, ALU.mult, ["u1", "ek", "par"], ["kt"])
                    else:
                        tt(kt[:, :], u1[:, :], ek[:, :], ALU.mult, ["u1", "ek"], ["kt"])
                    if not pass1:
                        act(u2[:, :], ps[0][:, :], AF.Silu, [("ps", 0)], ["u2"])
                        tt(qt[:, :], u2[:, :], eq[:, :], ALU.mult, ["u2", "eq"], ["qt"])
                if not pass1:
                    for j in range(nj):
                        mmgroup([(ps[5][:, :], wv[:, kc, 256 + j * 128:256 + (j + 1) * 128], hT[:, kc, :], kc == 0, kc == KC - 1) for kc in range(KC)],
                                [key, "hT"], [("ps", 5)])
                        act(sgb[j][:, :], ps[5][:, :], AF.Silu, [("ps", 5)], [("sgb", j)])
                transposes([(pstr[:, c * 128:(c + 1) * 128], kt[:, c * 128:(c + 1) * 128], identb[:, :]) for c in range(NCH)],
                           ["kt", "identb"], ["pstr"])
                cp(ktok[:, :, :], pstr[:, 0:NCH * 128].rearrange("p (c n) -> p c n", n=128), ["pstr"], ["ktok"], eng="act")
                po = [ps[3], ps[4]]
                pok = [("ps", 3), ("ps", 4)]

                def kvloc(c):
                    if gla:
                        return (ps[6], ("ps", 6), (c % 2) * 256) if c < 2 else (ps[2], ("ps", 2), (c % 2) * 256)
                    return (ps[6], ("ps", 6), c * 128)

                def vch_of(c):
                    return vtok[:, c, cb:cb + dv], akeys(c * 2048 + cb, c * 2048 + cb + dv)
                for c in range(NCH):
                    kb_, kk_, ko_ = kvloc(c)
                    vch, vkeys = vch_of(c)
                    mmgroup([(kb_[:, ko_:ko_ + dv], ktok[:, c, :], vch, True, True)], ["ktok"] + vkeys, [kk_])
                if not pass1:
                    for c in range(NCH):
                        csl = slice(c * CH, (c + 1) * CH)
                        mmgroup([(ps[5][:, csl], kt[:, csl], qt[:, csl], True, True)], ["kt", "qt"], [("ps", 5)])
                    for c in range(NCH):
                        csl = slice(c * CH, (c + 1) * CH)
                        tt(PTall[:, csl], ps[5][:, csl], maskb[:, :], ALU.mult, [("ps", 5), "maskb"], [("PT", c)])
                for c in range(NCH):
                    csl = slice(c * CH, (c + 1) * CH)
                    vch, vkeys = vch_of(c)
                    if not pass1:
                        Sbin = Sbf2[c % 2][:, cb:cb + dv]
                        for j in range(nj):
                            mmgroup([(po[j][:, csl], vch[:, j * 128:(j + 1) * 128], PTall[:, csl], True, False),
                                     (po[j][:, csl], Sbin[:, j * 128:(j + 1) * 128], qt[:, csl], False, True)],
                                    vkeys + [("PT", c), ("Sb", h, c % 2), "qt"], [pok[j]])
                    kb_, kk_, ko_ = kvloc(c)
                    T = Tst2[c % 2]
                    tk_ = ("Tst", c % 2)
                    a_ap = eq[:, c * CH + CH - 1:c * CH + CH]
                    tt(T[:, 0:dv], kb_[:, ko_:ko_ + dv], Sh, ALU.add, [kk_, sk], [tk_])
                    ts(Sh, T[:, 0:dv], a_ap, None, ALU.mult, None, [tk_, "eq"], [sk])
                    if not pass1:
                        act(Sbf2[(c + 1) % 2][:, cb:cb + dv], T[:, 0:dv], AF.Identity, [tk_, "eq"], [("Sb", h, (c + 1) % 2)], scale=a_ap)
                if pass1:
                    continue
                for j in range(nj):
                    act(sqb[j][:, :], po[j][:, :], AF.Square, [pok[j]], [("sqb", j)])
                    mmgroup([(ps[5][:, :], onesb[:, :], sqb[j][:, :], j == 0, j == nj - 1)], [("sqb", j), "onesb"], [("ps", 5)])
                compute_rstd(ps[5], 1.0 / dv, ("ps", 5))
                for j in range(nj):
                    tt(u2[:, :], po[j][:, :], rstd[:, :], ALU.mult, [pok[j], "rstd"], ["u2"])
                    nwc = par[:, P_GNW + j:P_GNW + j + 1] if gla else par[:, P_HNW:P_HNW + 1]
                    fc = cb // 128 + j
                    stt(ogT[:, fc, :], u2[:, :], nwc, sgb[j][:, :], ALU.mult, ALU.mult, ["u2", ("sgb", j), "par"], akeys(8192 + fc * 512, 8192 + fc * 512 + 512))
            if pass1:
                return
            pw = []

            def wissue(i):
                pw.append(wload([(lambda s: s[:, 0:KC * 512].rearrange("p (kc n) -> p kc n", n=512), wout_v[:, :, i * 512:(i + 1) * 512])], ckey=("wout", i)))
            wissue(0)
            wissue(1)
            for i in range(4):
                if i + 2 < 4:
                    wissue(i + 2)
                slot, key = pw[i]
                wv = wsl[slot][:, 0:KC * 512].rearrange("p (kc n) -> p kc n", n=512)
                for jj in range(4):
                    d = i * 4 + jj
                    pb = ps[d % 2]
                    kb = ("ps", d % 2)
                    mmgroup([(pb[:, :], wv[:, kc, jj * 128:(jj + 1) * 128], ogT[:, kc, :], kc == 0, kc == KC - 1) for kc in range(KC)],
                            [key] + akeys(8192, 16384), [kb])
                    stt(xT[:, d, :], pb[:, :], mod[:, 5 * 16 + d:5 * 16 + d + 1], xT[:, d, :], ALU.mult, ALU.add, [kb, ("xT", d), "mod"], [("xT", d)])

        npre = NPRE if stage >= 2 else 0
        for p in range(npre):
            load_x(p, xp_d)
            norm_mod(der[:, 0, :], modc(0))
            ffn(0, der[:, 3, :])
            norm_mod(der[:, 1, :], modc(3))
            mixer(p, True, par[:, P_PM + p:P_PM + p + 1])
        if stage >= 2:
            cp(Sbf2[0][:, :], Sall[:, 0:2048], [("S", h) for h in range(12)], [("Sb", h, 0) for h in range(12)])
        for t in range(NT):
            load_x(t)
            norm_mod(der[:, 0, :], modc(0))
            ffn(0, der[:, 3, :])
            if stage >= 2:
                norm_mod(der[:, 1, :], modc(3))
                mixer(t, False)
            if stage >= 3:
                norm_mod(der[:, 2, :], modc(6))
                ffn(1, der[:, 4, :])
            final_out(t)
        ykeys = [("y", t, s) for t in range(NT) for s in range(4)]
        S.op("sp", None, ykeys, [])
        S.emit(nc, stack)
    return nc


def _lay(v):
    v = np.asarray(v, np.float32)
    return np.ascontiguousarray(v.reshape(-1, 128).T)


def make_in_maps(inp):
    f32 = np.float32
    cst = np.zeros((128, NCST), f32)
    cst[:, C_ID:C_ID + 128] = np.eye(128, dtype=f32)
    cst[:, C_MU:C_MU + 128] = np.triu(np.ones((128, 128), f32))
    cst[:, C_ONE:C_ONE + 128] = 1.0
    sm = np.ones(512, f32)
    sm[::CH] = 0.0
    cst[:, C_SC:C_SC + 512] = sm[None, :]
    shared = {
        "cst": cst,
        "ada_w": np.ascontiguousarray(inp["ada_w"][0], f32),
        "wi1": np.ascontiguousarray(inp["ffn1_wi"][0], f32),
        "wo1": np.ascontiguousarray(inp["ffn1_wo"][0], f32),
        "wi2": np.ascontiguousarray(inp["ffn2_wi"][0], f32),
        "wo2": np.ascontiguousarray(inp["ffn2_wo"][0], f32),
        "w_in": np.ascontiguousarray(inp["w_in"][0], f32),
        "w2": np.ascontiguousarray(inp["gla_gate_w2"][0], f32),
        "w_out": np.ascontiguousarray(inp["w_out"][0], f32),
    }
    maps = []
    for core in range(8):
        b, seg = core // 4, core % 4
        par = np.zeros((128, NPAR), f32)
        par[:, P_C:P_C + 16] = _lay(inp["c"][b])
        par[:, P_ADAB:P_ADAB + 144] = _lay(inp["ada_b"][0])
        par[:, P_NW1:P_NW1 + 16] = _lay(inp["norm_ffn1_w"][0])
        par[:, P_NW2:P_NW2 + 16] = _lay(inp["norm_mix_w"][0])
        par[:, P_NW3:P_NW3 + 16] = _lay(inp["norm_ffn2_w"][0])
        par[:, P_NWF:P_NWF + 16] = _lay(inp["final_norm_w"])
        par[:, P_B2:P_B2 + 4] = _lay(inp["gla_gate_b2"][0])
        par[:, P_GNW:P_GNW + 2] = _lay(inp["gla_norm_w"][0])
        par[:, P_HNW:P_HNW + 1] = _lay(inp["hgrn_norm_w"][0])
        par[:, P_LB0:P_LB0 + 8] = _lay(inp["hgrn_lower_bounds"][0])
        par[:, P_LB1:P_LB1 + 8] = _lay(inp["hgrn_lower_bounds"][1])
        xp = np.zeros((NPRE * TT, D), f32)
        for p in range(NPRE):
            t0 = seg * TCORE - NPRE * TT + p * TT
            if t0 >= 0:
                par[:, P_PM + p] = 1.0
                xp[p * TT:(p + 1) * TT] = inp["x"][b, t0:t0 + TT, :]
        m = dict(shared)
        m["par"] = par
        m["xp"] = xp
        m["x"] = np.ascontiguousarray(inp["x"][b, seg * TCORE:(seg + 1) * TCORE, :], f32)
        maps.append(m)
    return maps


_NC_CACHE = {}


def kernel(**inputs):
    stage = 99
    if stage not in _NC_CACHE:
        _NC_CACHE[stage] = build(stage)
    nc = _NC_CACHE[stage]
    maps = make_in_maps(inputs)
    res = run_bass_kernel_spmd(nc, maps, core_ids=list(range(8)))
    out = np.empty((2, 8192, D), np.float32)
    for core in range(8):
        b, seg = core // 4, core % 4
        out[b, seg * TCORE:(seg + 1) * TCORE, :] = res.results[core]["y"]
    return out
```
